# Optimizing a Trainium2 kernel written in Bass

```python
import jax
import jax.numpy as jnp
from jax import lax
import numpy as np

D_MODEL = 2048
BATCH = 2
SEQ = 4096
DEPTH = 2
DEC_BATCH = 8
DEC_SEQ = 4
PAST_LEN = 16384
PAGE_SIZE = 128

N_A_LAYERS = DEPTH // 2
N_B_LAYERS = DEPTH - N_A_LAYERS
CHUNK = 128
GMLP_WIDTH = D_MODEL
GMLP_GROUPS = 16
GMLP_GROUP_DIM = GMLP_WIDTH // GMLP_GROUPS
N_HEADS = 16
HEAD_DIM = D_MODEL // N_HEADS
SB_BLOCK = 128
SB_LOGIT_OFFSET = -8.0
PEER_HEADS = 8
PEER_NKEYS = 128
PEER_EXPERTS = PEER_NKEYS * PEER_NKEYS
PEER_KEY_DIM = 256
PEER_HALF = PEER_KEY_DIM // 2
PEER_TOPK = 16
PEER_TOKEN_BLOCK = 128
N_MOD = 6
EPS = 1e-6

kernel_name = 'yoco_gmlp_stickbreak_peer_step'


def rms_norm(x, g):
    xf = x.astype(jnp.float32)
    y = xf * lax.rsqrt(jnp.mean(xf * xf, axis=-1, keepdims=True) + EPS)
    return (y * g.astype(jnp.float32)).astype(x.dtype)


def modulate(h, shift, scale):
    return h * (1 + scale[:, None, :]) + shift[:, None, :]


def adaln(c, w, b, n):
    m = jax.nn.silu(c) @ w + b
    return jnp.split(m, n, axis=-1)


def gmlp_mixer(h, w_in, b_in, v_g, w_s, b_s, w_out):
    B, S, _ = h.shape
    z = jax.nn.gelu(h @ w_in + b_in, approximate=False)
    u, v = jnp.split(z, 2, axis=-1)
    v = rms_norm(v, v_g)
    pad = (-S) % CHUNK
    vp = jnp.pad(v, ((0, 0), (0, pad), (0, 0)))
    n_chunks = (S + pad) // CHUNK
    vc = vp.reshape(B, n_chunks, CHUNK, GMLP_GROUPS, GMLP_GROUP_DIM)
    causal = jnp.tril(jnp.ones((CHUNK, CHUNK), dtype=bool))
    w_mask = jnp.where(causal[None], w_s, jnp.zeros_like(w_s))
    mixed = jnp.einsum('gts,bnsgd->bntgd', w_mask, vc) + b_s.T[None, None, :, :, None]
    mixed = mixed.reshape(B, n_chunks * CHUNK, GMLP_WIDTH)[:, :S]
    return (u * mixed) @ w_out, v


def stick_breaking_block(q, k, v, q_pos, k_pos, logit_bias):
    z = jnp.einsum('bqhd,bkhd->bhqk', q, k, preferred_element_type=jnp.float32) * (HEAD_DIM ** -0.5)
    z = z + logit_bias.astype(jnp.float32)[None, :, None, None]
    mask = k_pos[None, :] < q_pos[:, None]
    log_1m = jnp.where(mask, jax.nn.log_sigmoid(-z), 0.0)
    log_surv = lax.cumsum(log_1m, axis=3, reverse=True) - log_1m
    a = jnp.where(mask, jnp.exp(jax.nn.log_sigmoid(z) + log_surv), 0.0)
    return jnp.einsum('bhqk,bkhd->bqhd', a.astype(v.dtype), v)


def stick_breaking(q, k, v, q_start, logit_bias):
    B, Tq, H, d = q.shape
    blk = min(SB_BLOCK, Tq)
    pad = (-Tq) % blk
    n_blk = (Tq + pad) // blk
    qp = jnp.pad(q, ((0, 0), (0, pad), (0, 0), (0, 0)))
    qb = qp.reshape(B, n_blk, blk, H, d).transpose(1, 0, 2, 3, 4)
    q_pos = (q_start + jnp.arange(n_blk * blk)).reshape(n_blk, blk)
    k_pos = jnp.arange(k.shape[1])
    out = lax.map(lambda a: stick_breaking_block(a[0], k, v, a[1], k_pos, logit_bias), (qb, q_pos))
    return out.transpose(1, 0, 2, 3, 4).reshape(B, n_blk * blk, H, d)[:, :Tq]


def shared_kv(x, c, mod_w, mod_b, norm_g, w_kv, k_norm_g):
    B, S, _ = x.shape
    shift, scale = adaln(c, mod_w, mod_b, 2)
    h = modulate(rms_norm(x, norm_g), shift, scale)
    kv = (h @ w_kv).reshape(B, S, 2, N_HEADS, HEAD_DIM)
    return rms_norm(kv[:, :, 0], k_norm_g), kv[:, :, 1]


def peer(h, w_q, subkeys, u_tab, v_tab):
    B, S, D = h.shape
    T = B * S
    xf = h.reshape(T, D)
    q = (xf @ w_q).reshape(T, PEER_HEADS, 2, PEER_HALF)
    s = jnp.einsum('thpc,pnc->thpn', q, subkeys, preferred_element_type=jnp.float32)
    s1, i1 = lax.top_k(s[:, :, 0], PEER_TOPK)
    s2, i2 = lax.top_k(s[:, :, 1], PEER_TOPK)
    n_cand = PEER_TOPK * PEER_TOPK
    cand = (s1[..., :, None] + s2[..., None, :]).reshape(T, PEER_HEADS, n_cand)
    cidx = (i1[..., :, None] * PEER_NKEYS + i2[..., None, :]).reshape(T, PEER_HEADS, n_cand)
    top_s, pos = lax.top_k(cand, PEER_TOPK)
    expert = jnp.take_along_axis(cidx, pos, axis=-1)
    gate = jax.nn.softmax(top_s, axis=-1).astype(h.dtype)
    blk = min(PEER_TOKEN_BLOCK, T)
    pad = (-T) % blk
    nb = (T + pad) // blk
    xb = jnp.pad(xf, ((0, pad), (0, 0))).reshape(nb, blk, D)
    eb = jnp.pad(expert, ((0, pad), (0, 0), (0, 0))).reshape(nb, blk, PEER_HEADS, PEER_TOPK)
    gb = jnp.pad(gate, ((0, pad), (0, 0), (0, 0))).reshape(nb, blk, PEER_HEADS, PEER_TOPK)

    def block(args):
        xt, et, gt = args
        act = jax.nn.gelu(jnp.einsum('td,thkd->thk', xt, u_tab[et]), approximate=False)
        return jnp.einsum('thk,thkd->td', gt * act, v_tab[et])

    out = lax.map(block, (xb, eb, gb)).reshape(nb * blk, D)[:T]
    return out.reshape(B, S, D)


def forward(x, c, past_k, past_v, p):
    B, S, _ = x.shape
    q_start = 0 if past_k is None else past_k.shape[1]
    gmlp_v = []
    k_new = v_new = k_all = v_all = None
    for layer in range(DEPTH):
        sh_m, sc_m, g_m, sh_f, sc_f, g_f = adaln(c, p['mod_w'][layer], p['mod_b'][layer], N_MOD)
        h = modulate(rms_norm(x, p['norm_mix_g'][layer]), sh_m, sc_m)
        if layer < N_A_LAYERS:
            mix, v_rows = gmlp_mixer(h, p['a_w_in'][layer], p['a_b_in'][layer], p['a_v_norm_g'][layer],
                                     p['a_w_s'][layer], p['a_b_s'][layer], p['a_w_out'][layer])
            gmlp_v.append(v_rows)
        else:
            i = layer - N_A_LAYERS
            q = rms_norm((h @ p['b_w_q'][i]).reshape(B, S, N_HEADS, HEAD_DIM), p['b_q_norm_g'][i])
            o = stick_breaking(q, k_all, v_all, q_start, p['b_logit_bias'][i])
            mix = o.reshape(B, S, N_HEADS * HEAD_DIM) @ p['b_w_o'][i]
        x = x + g_m[:, None, :] * mix
        h = modulate(rms_norm(x, p['norm_ffn_g'][layer]), sh_f, sc_f)
        x = x + g_f[:, None, :] * peer(h, p['peer_w_q'][layer], p['peer_subkeys'][layer],
                                       p['peer_u'][layer], p['peer_v'][layer])
        if layer == N_A_LAYERS - 1:
            k_new, v_new = shared_kv(x, c, p['kv_mod_w'], p['kv_mod_b'], p['kv_norm_g'],
                                     p['kv_w'], p['k_norm_g'])
            if past_k is None:
                k_all, v_all = k_new, v_new
            else:
                k_all = jnp.concatenate([past_k, k_new], axis=1)
                v_all = jnp.concatenate([past_v, v_new], axis=1)
    return x, k_new, v_new, jnp.stack(gmlp_v)


def setup_inputs(seed: int = 0):
    key = jax.random.key(seed)
    ks = iter(jax.random.split(key, 48))

    def nrm(shape, scale):
        return jax.random.normal(next(ks), shape, jnp.float32) * scale

    def gain(shape):
        return 1.0 + nrm(shape, 0.02)

    d = D_MODEL
    n_pages = PAST_LEN // PAGE_SIZE
    n_pool = (DEC_BATCH * n_pages * 5) // 4
    perm = jax.random.permutation(next(ks), n_pool)[:DEC_BATCH * n_pages]
    page_table = perm.reshape(DEC_BATCH, n_pages).astype(jnp.int32)
    return {
        'x_prompt': nrm((BATCH, SEQ, d), 1.0),
        'x_sample': nrm((DEC_BATCH, DEC_SEQ, d), 1.0),
        'cache_k': nrm((n_pool, PAGE_SIZE, N_HEADS, HEAD_DIM), 1.0),
        'cache_v': nrm((n_pool, PAGE_SIZE, N_HEADS, HEAD_DIM), 1.0),
        'page_table': page_table,
        'c_prompt': nrm((BATCH, d), 1.0),
        'c_sample': nrm((DEC_BATCH, d), 1.0),
        'mod_w': nrm((DEPTH, d, N_MOD * d), 0.3 * d ** -0.5),
        'mod_b': nrm((DEPTH, N_MOD * d), 0.01),
        'norm_mix_g': gain((DEPTH, d)),
        'norm_ffn_g': gain((DEPTH, d)),
        'a_w_in': nrm((N_A_LAYERS, d, 2 * GMLP_WIDTH), d ** -0.5),
        'a_b_in': nrm((N_A_LAYERS, 2 * GMLP_WIDTH), 0.01),
        'a_v_norm_g': gain((N_A_LAYERS, GMLP_WIDTH)),
        'a_w_s': nrm((N_A_LAYERS, GMLP_GROUPS, CHUNK, CHUNK), CHUNK ** -0.5),
        'a_b_s': gain((N_A_LAYERS, GMLP_GROUPS, CHUNK)),
        'a_w_out': nrm((N_A_LAYERS, GMLP_WIDTH, d), GMLP_WIDTH ** -0.5),
        'kv_mod_w': nrm((d, 2 * d), 0.3 * d ** -0.5),
        'kv_mod_b': nrm((2 * d,), 0.01),
        'kv_norm_g': gain((d,)),
        'kv_w': nrm((d, 2 * N_HEADS * HEAD_DIM), d ** -0.5),
        'k_norm_g': gain((HEAD_DIM,)),
        'b_w_q': nrm((N_B_LAYERS, d, N_HEADS * HEAD_DIM), d ** -0.5),
        'b_q_norm_g': gain((N_B_LAYERS, HEAD_DIM)),
        'b_logit_bias': SB_LOGIT_OFFSET + nrm((N_B_LAYERS, N_HEADS), 0.1),
        'b_w_o': nrm((N_B_LAYERS, N_HEADS * HEAD_DIM, d), (N_HEADS * HEAD_DIM) ** -0.5),
        'peer_w_q': nrm((DEPTH, d, PEER_HEADS * PEER_KEY_DIM), d ** -0.5),
        'peer_subkeys': nrm((DEPTH, 2, PEER_NKEYS, PEER_HALF), PEER_HALF ** -0.5),
        'peer_u': nrm((DEPTH, PEER_EXPERTS, d), d ** -0.5),
        'peer_v': nrm((DEPTH, PEER_EXPERTS, d), PEER_HEADS ** -0.5),
    }


def reference(x_prompt, x_sample, cache_k, cache_v, page_table, c_prompt, c_sample,
              mod_w, mod_b, norm_mix_g, norm_ffn_g,
              a_w_in, a_b_in, a_v_norm_g, a_w_s, a_b_s, a_w_out,
              kv_mod_w, kv_mod_b, kv_norm_g, kv_w, k_norm_g,
              b_w_q, b_q_norm_g, b_logit_bias, b_w_o,
              peer_w_q, peer_subkeys, peer_u, peer_v):
    p = {
        'mod_w': mod_w, 'mod_b': mod_b, 'norm_mix_g': norm_mix_g, 'norm_ffn_g': norm_ffn_g,
        'a_w_in': a_w_in, 'a_b_in': a_b_in, 'a_v_norm_g': a_v_norm_g, 'a_w_s': a_w_s,
        'a_b_s': a_b_s, 'a_w_out': a_w_out,
        'kv_mod_w': kv_mod_w, 'kv_mod_b': kv_mod_b, 'kv_norm_g': kv_norm_g, 'kv_w': kv_w,
        'k_norm_g': k_norm_g,
        'b_w_q': b_w_q, 'b_q_norm_g': b_q_norm_g, 'b_logit_bias': b_logit_bias, 'b_w_o': b_w_o,
        'peer_w_q': peer_w_q, 'peer_subkeys': peer_subkeys, 'peer_u': peer_u, 'peer_v': peer_v,
    }
    y_prompt, k_prompt, v_prompt, _ = forward(x_prompt, c_prompt, None, None, p)
    n_seq = page_table.shape[0]
    past_k = cache_k[page_table].reshape(n_seq, -1, N_HEADS, HEAD_DIM)
    past_v = cache_v[page_table].reshape(n_seq, -1, N_HEADS, HEAD_DIM)
    y_sample, k_sample, v_sample, gmlp_v_sample = forward(x_sample, c_sample, past_k, past_v, p)
    return (y_prompt, y_sample, k_prompt, v_prompt, k_sample, v_sample, gmlp_v_sample)
```

```python
from contextlib import ExitStack
import numpy as np
import concourse.bass as bass
import concourse.mybir as mybir
from concourse.bass_utils import run_bass_kernel_spmd

F32 = mybir.dt.float32
F32R = mybir.dt.float32r
I32 = mybir.dt.int32
U32 = mybir.dt.uint32
ALU = mybir.AluOpType
AF = mybir.ActivationFunctionType
AX = mybir.AxisListType

D = 2048
KT = 16
NT = 9
NP = 32
NA = NP + 1
NB = 256
HB = 128
EPS = 1e-6
NEG = -1.0e30
SAME_ENGINE_RAW_SYNC = True


def _key(x):
    if isinstance(x, str):
        return x
    n = getattr(x, "name", None)
    if isinstance(n, str):
        return n
    return x.tensor.name


class Em:
    def __init__(self, nc, es):
        self.nc, self.es = nc, es
        self.engs = ["pe", "act", "dve", "pool", "sp"]
        self.ops = {e: [] for e in self.engs}
        self.psem = {e: es.enter_context(nc.semaphore("prog_" + e)) for e in self.engs}
        self.cnt = {e: 0 for e in self.engs}
        self.waited = {}
        self.bufs = {}
        self.chan = {}
        self.untracked = set()

    def sb(self, name, shape, dt=F32):
        return self.es.enter_context(self.nc.sbuf_tensor(name, list(shape), dt))

    def ps(self, name, shape, dt=F32):
        return self.es.enter_context(self.nc.psum_tensor(name, list(shape), dt))

    def _st(self, k):
        return self.bufs.setdefault(k, {"w": None, "r": []})

    def _tokval(self, tok):
        if tok[0] == "eng":
            return ("p", tok[1]), self.psem[tok[1]], tok[2]
        ch = self.chan[tok[1]]
        return ("c", tok[1]), ch[0], ch[1]

    def _rel(self, k):
        if "#" in k:
            return [k, k.split("#")[0]], [k]
        subs = [q for q in self.bufs if q.startswith(k + "#")]
        return [k] + subs, [k] + subs

    def _deps(self, eng, R, W):
        R = [q for k in R if k not in self.untracked for q in self._rel(k)[0]]
        W = [q for k in W for q in self._rel(k)[0]]
        waits = []

        def need(tok, raw):
            if tok is None:
                return
            if tok[0] == "eng" and tok[1] == eng:
                if not (raw and SAME_ENGINE_RAW_SYNC and eng in ("act", "dve", "pool")):
                    return
            sk, sem, val = self._tokval(tok)
            if self.waited.get((eng, sk), 0) >= val:
                return
            self.waited[(eng, sk)] = val
            waits.append((sem, val))

        for k in R:
            if k in self.untracked:
                continue
            need(self._st(k)["w"], True)
        for k in W:
            st = self._st(k)
            need(st["w"], False)
            for t in st["r"]:
                need(t, False)
        return waits

    def _commit(self, tok, R, W):
        R = [q for k in R if k not in self.untracked for q in self._rel(k)[1]]
        W = [q for k in W for q in self._rel(k)[1]]
        for k in R:
            if k in self.untracked:
                continue
            self._st(k)["r"].append(tok)
        for k in W:
            st = self._st(k)
            st["w"] = tok
            st["r"] = []

    def op(self, eng, fn, R=(), W=()):
        R = [_key(x) for x in R]
        W = [_key(x) for x in W]
        waits = self._deps(eng, R, W)
        self.cnt[eng] += 1
        tok = ("eng", eng, self.cnt[eng])
        self.ops[eng].append((waits, fn, (self.psem[eng], 1)))
        self._commit(tok, R, W)

    def dma(self, q, out, in_, R=None, W=None, fn=None, ckey=None):
        Rk = [_key(x) for x in (R if R is not None else [in_])]
        Wk = [_key(x) for x in (W if W is not None else [out])]
        kind = "sw" if q == "pool" else "hw"
        ck = (ckey or (Wk[0] if Wk[0] not in self.untracked_out else Rk[0]), kind)
        if ck not in self.chan:
            self.chan[ck] = [self.es.enter_context(self.nc.semaphore("ch%d" % len(self.chan))), 0]
        waits = self._deps(q, Rk, Wk)
        ch = self.chan[ck]
        ch[1] += 16
        if fn is None:
            fn = lambda e, o=out, i=in_: e.dma_start(out=o, in_=i)
        self.ops[q].append((waits, fn, (ch[0], 16)))
        self._commit(("dma", ck), Rk, Wk)

    untracked_out = set()

    def wait_all_dma(self, eng="sp"):
        waits = []
        for ck, ch in self.chan.items():
            if ch[1] > 0:
                waits.append((ch[0], ch[1]))
        self.ops[eng].append((waits, None, None))

    def replay(self):
        with self.nc.Block() as blk:
            def run(e, name):
                for waits, fn, inc in self.ops[name]:
                    for sem, val in waits:
                        e.wait_ge(sem, val)
                    if fn is None:
                        continue
                    ins = fn(e)
                    if inc is not None:
                        ins.then_inc(inc[0], inc[1])

            @blk.tensor
            def _(e):
                run(e, "pe")

            @blk.scalar
            def _(e):
                run(e, "act")

            @blk.vector
            def _(e):
                run(e, "dve")

            @blk.gpsimd
            def _(e):
                run(e, "pool")

            @blk.sync
            def _(e):
                run(e, "sp")


def build_program():
    nc = bass.Bass("TRN2", target_bir_lowering=False)
    es = ExitStack()
    em = Em(nc, es)

    def din(name, shape, dt=F32):
        t = nc.dram_tensor(name, list(shape), dt, kind="ExternalInput")
        em.untracked.add(name)
        return t.ap()

    def dout(name, shape, dt=F32):
        return nc.dram_tensor(name, list(shape), dt, kind="ExternalOutput").ap()

    xall = din("xall", [NP * 128, D])
    xsmp = din("xsmp", [128, D])
    c2T = din("c2T", [128, KT * 2])
    pt_d = din("pt", [1, 128], I32)
    myrows_d = din("myrows", [128, 8], I32)
    amask_d = din("amask", [128, 4 * 128])
    smask_d = din("smask", [128, 64])
    umat_d = din("umat", [128, 128])
    ones_d = din("ones128", [128, 128])
    biasp_d = din("biasrow_p", [1, D])
    biass_d = din("biasrow_s", [1, 64])
    iota_d = din("iota_p", [128, 1])
    mod_w = din("mod_w", [2, D, 6 * D])
    mod_bT = din("mod_bT", [2, 128, 96])
    mod_b = din("mod_b", [2, 6 * D])
    normT = din("normT", [128, 5 * KT])
    a_w_in = din("a_w_in", [D, 2 * D])
    a_b_in = din("a_b_in", [1, 2 * D])
    a_vg = din("a_vg", [1, D])
    wsT_d = din("wsT", [128, 16 * 128])
    bsT_d = din("bsT", [128, 16])
    a_w_out = din("a_w_out", [D, D])
    kv_mod_w = din("kv_mod_w", [D, 2 * D])
    kv_mod_bT = din("kv_mod_bT", [128, 32])
    kv_w = din("kv_w", [D, 2 * D])
    kng_d = din("k_norm_g", [1, 128])
    qng_d = din("q_norm_g", [1, 128])
    b_w_q = din("b_w_q", [D, D])
    b_w_o = din("b_w_o", [D, D])
    peer_w_q = din("peer_w_q", [2, D, D])
    skT_d = din("skT", [2, 128, 2 * 128])
    peer_u = din("peer_u", [2 * 16384, D])
    peer_v = din("peer_v", [2 * 16384, D])
    cache_k = din("cache_k", [1280 * 128, D])
    cache_v = din("cache_v", [1280 * 128, D])
    ident_d = din("ident", [128, 128])
    tril_d = din("tril", [128, 128])

    y_o = dout("y", [NT * 128, D])
    k_o = dout("ko", [NA * 128, D])
    v_o = dout("vo", [NA * 128, D])
    gv_o = dout("gv", [128, D])
    kv_all = nc.dram_tensor("kv_all", [NA * 128, 2 * D], F32).ap()
    xs_all = nc.dram_tensor("xs_all", [NA * 128, D], F32).ap()

    x = em.sb("x", [128, D])
    tmA = em.sb("tmA", [128, D])
    tmB = em.sb("tmB", [128, D])
    tmC = em.sb("tmC", [128, D])
    hT = em.sb("hT", [128, KT, 128], F32R)
    hT32 = hT[:].bitcast(F32)
    z = em.sb("z", [128, 2 * D])
    wbuf = [em.sb("wbuf%d" % i, [128, KT + 1, NB]) for i in range(2)]
    wr = em.sb("wr", [128, KT + 1, HB], F32R)
    wsT = em.sb("wsTs", [128, 16, 128])
    vg_rep = em.sb("vg_rep", [128, D])
    grep = [em.sb("grep%d" % i, [128, D]) for i in range(2)]
    cand = em.sb("cand", [128, 8, 256])
    cidx = em.sb("cidx", [128, 8, 256])
    gb = [em.sb("gb%d" % i, [128, D]) for i in range(4)]
    ident = em.sb("ident_s", [128, 128])
    ones_row = em.sb("ones_row", [1, 128])
    ones_row_r = em.sb("ones_row_r", [1, 128], F32R)
    scT = em.sb("scT", [128, KT * 2])
    modT = [em.sb("modT%d" % l, [128, 96 * 2]) for l in range(2)]
    kvmodT = em.sb("kvmodT", [128, 32 * 2])
    mbT = [em.sb("mbT%d" % l, [128, 96]) for l in range(2)]
    kvmbT = em.sb("kvmbT", [128, 32])
    nrmT = em.sb("nrmT", [128, 5 * KT])
    coefA = em.sb("coefA", [128, 5 * 2 * KT])
    coefB = em.sb("coefB", [128, 5 * 2 * KT])
    bsT = em.sb("bsTs", [128, 16])
    kng = em.sb("kng", [128, 128])
    qng = em.sb("qng", [128, 128])
    skT = [em.sb("skT%d" % l, [128, 256]) for l in range(2)]
    small = em.sb("small", [128, 64])
    top = em.sb("top", [128, 16, 16])
    tidx = em.sb("tidx", [128, 16, 16], U32)
    tidxf = em.sb("tidxf", [128, 16, 16])
    mr = em.sb("mr", [128, 256])
    tops = em.sb("tops", [128, 8, 16])
    eidf = em.sb("eidf", [128, 128])
    eid = em.sb("eid", [128, 128], I32)
    gate = em.sb("gate", [128, 128])
    actv = em.sb("actv", [128, 128])
    ga = em.sb("ga", [128, 128])
    zsum = em.sb("zsum", [128, 16])
    att_t = em.sb("att_t", [128, 2 * 512])
    lacc = em.sb("lacc", [128, 512])
    amask = em.sb("amask_s", [128, 4 * 128])
    smask = em.sb("smask_s", [128, 64])
    umat = em.sb("umat_s", [128, 128])
    ones128 = em.sb("ones128_s", [128, 128])
    biasg = em.sb("biasg", [1, 512])
    biass = em.sb("biass", [1, 64])
    iota_p = em.sb("iota_ps", [128, 1])
    pidx = em.sb("pidx", [128, 128], I32)
    myrows = em.sb("myrows_s", [128, 8], I32)

    psL = [em.ps("psL%d" % i, [128, 512]) for i in range(2)]
    psT = [em.ps("psT%d" % i, [128, 512]) for i in range(2)]
    psW = [em.ps("psW%d" % i, [128, 512]) for i in range(4)]

    st = {"wslot": 0, "pl": 0, "pt": 0}

    def load_w(wap, n0, nb, bias_ap=None):
        s = st["wslot"]
        st["wslot"] ^= 1
        wb = wbuf[s]
        em.dma("sp", wb[:, 0:KT, 0:nb], wap[:, n0:n0 + nb].rearrange("(kt p) n -> p kt n", p=128),
               R=[], W=[wb])
        if bias_ap is not None:
            em.dma("sp", wb[0:1, KT, 0:nb], bias_ap[0:1, n0:n0 + nb], R=[], W=[wb])
        return wb

    def linear(wap, ncols, consume, bias_ap=None, lhs=None):
        r32 = lhs is None
        lhs = lhs if lhs is not None else hT
        nblk = ncols // NB
        nxt = load_w(wap, 0, NB, bias_ap)
        for b in range(nblk):
            wb = nxt
            if b + 1 < nblk:
                nxt = load_w(wap, (b + 1) * NB, NB, bias_ap)
            p = psL[st["pl"]]
            st["pl"] ^= 1
            if not r32:
                for kt in range(KT):
                    em.op("pe", lambda e, p=p, wb=wb, kt=kt: e.matmul(
                        p[:, 0:NB], lhsT=lhs[:, kt, :], rhs=wb[:, kt, 0:NB], start=(kt == 0), stop=(kt == KT - 1)),
                        R=[lhs, wb], W=[p])
            else:
                for h0 in range(0, NB, HB):
                    em.op("act", lambda e, wb=wb, h0=h0: e.copy(out=wr[:, 0:KT, :], in_=wb[:, 0:KT, h0:h0 + HB]),
                          R=[wb], W=[wr])
                    if bias_ap is not None:
                        em.op("act", lambda e, wb=wb, h0=h0: e.copy(out=wr[0:1, KT, :], in_=wb[0:1, KT, h0:h0 + HB]),
                              R=[wb], W=[wr])
                    for kt in range(KT):
                        last = (kt == KT - 1) and bias_ap is None
                        em.op("pe", lambda e, p=p, h0=h0, kt=kt, last=last: e.matmul(
                            p[:, h0:h0 + HB], lhsT=lhs[:, kt, :], rhs=wr[:, kt, :], start=(kt == 0), stop=last),
                            R=[lhs, wr], W=[p])
                    if bias_ap is not None:
                        em.op("pe", lambda e, p=p, h0=h0: e.matmul(
                            p[:, h0:h0 + HB], lhsT=ones_row_r[0:1, :], rhs=wr[0:1, KT, :], start=False, stop=True),
                            R=[ones_row_r, wr], W=[p])
            consume(p, b * NB, NB)

    def transpose_blocks(nblk, src_fn, srckeys, dst_fn, dstkey, coef=None):
        for g0 in range(0, nblk, 4):
            n = min(4, nblk - g0)
            p = psT[st["pt"]]
            st["pt"] ^= 1
            for i in range(n):
                em.op("pe", lambda e, p=p, i=i, blk=g0 + i: e.transpose(
                    p[:, i * 128:(i + 1) * 128], src_fn(blk), ident[:]), R=list(srckeys) + [ident], W=[p])
            if coef is None:
                em.op("act", lambda e, p=p, g0=g0, n=n: e.copy(
                    out=dst_fn(g0, n), in_=p[:, 0:n * 128].rearrange("p (a b) -> p a b", a=n)),
                    R=[p], W=[dstkey])
            else:
                A, B = coef
                for i in range(n):
                    kt = g0 + i
                    em.op("act", lambda e, p=p, i=i, kt=kt, A=A, B=B: e.activation(
                        out=dst_fn(kt, 1)[:, 0, :], in_=p[:, i * 128:(i + 1) * 128], func=AF.Identity,
                        bias=B[:, kt:kt + 1], scale=A[:, kt:kt + 1]),
                        R=[p, coefA, coefB], W=[dstkey])

    def transpose_to(dst, src, coef=None, r32=True):
        dv = hT
        transpose_blocks(KT, lambda b: src[:, b * 128:(b + 1) * 128], [src],
                         lambda g0, n: dv[:, g0:g0 + n, :], hT, coef)

    def transpose_back(dst, srcT):
        srcT = hT32
        transpose_blocks(KT, lambda b: srcT[:, b, :], ["hT"],
                         lambda g0, n: dst[:, g0 * 128:(g0 + n) * 128].rearrange("p (a b) -> p a b", a=n), dst)

    def rsqrt_cols(c0, c1, n):
        em.op("dve", lambda e: e.tensor_scalar(out=small[:, c0:c1], in0=small[:, c0:c1],
                                               scalar1=1.0 / n, scalar2=EPS, op0=ALU.mult, op1=ALU.add),
              R=[small], W=[small])
        em.op("act", lambda e: e.activation(out=small[:, c0:c1], in_=small[:, c0:c1], func=AF.Sqrt),
              R=[small], W=[small])
        em.op("dve", lambda e: e.reciprocal(out=small[:, c0:c1], in_=small[:, c0:c1]), R=[small], W=[small])

    def rstd_of(src, srckey, col, n):
        em.op("dve", lambda e: e.memset(small[:, col:col + 1], 0.0), W=[small])
        em.op("act", lambda e: e.activation(out=tmC[:, 0:n], in_=src, func=AF.Square,
                                             accum_out=small[:, col:col + 1]), R=[srckey], W=[tmC, small])
        rsqrt_cols(col, col + 1, n)

    def norm_mod_T(kind, which):
        rstd_of(x[:], x, 0, D)
        em.op("dve", lambda e: e.tensor_scalar(out=tmA[:], in0=x[:], scalar1=small[:, 0:1], scalar2=None,
                                               op0=ALU.mult), R=[x, small], W=[tmA])
        o = (kind * 2 + which) * KT
        transpose_to(hT, tmA, coef=(coefA[:, o:o + KT], coefB[:, o:o + KT]))

    def head_norm(src, srckey, gain, dst, extra=1.0):
        d3 = dst[:].rearrange("p (h d) -> p h d", d=128)
        em.op("dve", lambda e: e.tensor_tensor(out=dst[:], in0=src, in1=src, op=ALU.mult), R=[srckey], W=[dst])
        em.op("dve", lambda e: e.tensor_reduce(out=small[:, 16:32], in_=d3, axis=AX.X, op=ALU.add), R=[dst], W=[small])
        rsqrt_cols(16, 32, 128)
        if extra != 1.0:
            em.op("dve", lambda e: e.tensor_scalar(out=small[:, 16:32], in0=small[:, 16:32], scalar1=extra, scalar2=None,
                                                   op0=ALU.mult), R=[small], W=[small])
        em.op("dve", lambda e: e.tensor_tensor(out=d3, in0=src.rearrange("p (h d) -> p h d", d=128),
                                               in1=small[:, 16:32].unsqueeze(2).to_broadcast([128, 16, 128]),
                                               op=ALU.mult), R=[srckey, small], W=[dst])
        em.op("dve", lambda e: e.tensor_tensor(out=d3, in0=d3, in1=gain[:].unsqueeze(1).to_broadcast([128, 16, 128]),
                                               op=ALU.mult), R=[dst, gain], W=[dst])

    em.dma("sp", tmB[:, 0:128], tril_d, R=[], W=[tmB])
    for dst, src in ((ident, ident_d), (scT, c2T), (nrmT, normT), (bsT, bsT_d),
                     (kvmbT, kv_mod_bT), (mbT[0], mod_bT[0]), (mbT[1], mod_bT[1]),
                     (skT[0], skT_d[0]), (skT[1], skT_d[1]), (amask, amask_d), (smask, smask_d),
                     (umat, umat_d), (ones128, ones_d), (biass, biass_d),
                     (iota_p, iota_d), (myrows, myrows_d)):
        em.dma("sp", dst[:], src, R=[], W=[dst])
    em.dma("sp", wsT[:].rearrange("p g t -> p (g t)"), wsT_d, R=[], W=[wsT])
    em.dma("sp", vg_rep[:], a_vg.partition_broadcast(128), R=[], W=[vg_rep])
    em.dma("sp", kng[:], kng_d.partition_broadcast(128), R=[], W=[kng])
    em.dma("sp", qng[:], qng_d.partition_broadcast(128), R=[], W=[qng])
    em.dma("sp", eid[:], pt_d.partition_broadcast(128), R=[], W=[eid])
    em.op("dve", lambda e: e.memset(ones_row[:], 1.0), W=[ones_row])
    em.op("act", lambda e: e.copy(out=ones_row_r[:], in_=ones_row[:]), R=[ones_row], W=[ones_row_r])
    em.op("dve", lambda e: e.tensor_copy(out=eidf[:], in_=eid[:]), R=[eid], W=[eidf])
    em.op("dve", lambda e: e.tensor_scalar(out=pidx[:], in0=eidf[:], scalar1=128.0, scalar2=iota_p[:, 0:1],
                                           op0=ALU.mult, op1=ALU.add), R=[eidf, iota_p], W=[pidx])
    em.op("dve", lambda e: e.tensor_tensor(out=wsT[:], in0=wsT[:],
                                           in1=tmB[:, 0:128].unsqueeze(1).to_broadcast([128, 16, 128]), op=ALU.mult),
          R=[wsT, tmB], W=[wsT])
    em.op("act", lambda e: e.activation(out=scT[:], in_=scT[:], func=AF.Silu), R=[scT], W=[scT])

    def mod_T(wap, ntiles, outT, biasT):
        pm = psW[0]
        nblk = ntiles * 128 // NB
        nxt = load_w(wap, 0, NB)
        for b in range(nblk):
            wb = nxt
            if b + 1 < nblk:
                nxt = load_w(wap, (b + 1) * NB, NB)
            for i in range(NB // 128):
                nt = b * (NB // 128) + i
                for kt in range(KT):
                    em.op("pe", lambda e, wb=wb, i=i, nt=nt, kt=kt: e.matmul(
                        pm[:, nt * 2:nt * 2 + 2], lhsT=wb[:, kt, i * 128:(i + 1) * 128],
                        rhs=scT[:, kt * 2:kt * 2 + 2], start=(kt == 0), stop=(kt == KT - 1)),
                        R=[wb, scT], W=[pm])
        em.op("dve", lambda e: e.tensor_tensor(
            out=outT[:].rearrange("p (n w) -> p n w", w=2), in0=pm[:, 0:ntiles * 2].rearrange("p (n w) -> p n w", w=2),
            in1=biasT[:].unsqueeze(2).to_broadcast([128, ntiles, 2]), op=ALU.add), R=[pm, biasT], W=[outT])

    mod_T(mod_w[0], 96, modT[0], mbT[0])
    mod_T(kv_mod_w, 32, kvmodT, kvmbT)
    mod_T(mod_w[1], 96, modT[1], mbT[1])

    def coefs(kind, mT, sh0, sc0):
        for w in range(2):
            o = (kind * 2 + w) * KT
            mv = mT[:].rearrange("p (n w) -> p n w", w=2)
            em.op("dve", lambda e, o=o, w=w, mv=mv: e.scalar_tensor_tensor(
                out=coefA[:, o:o + KT], in0=mv[:, sc0:sc0 + KT, w], scalar=1.0,
                in1=nrmT[:, kind * KT:(kind + 1) * KT], op0=ALU.add, op1=ALU.mult),
                R=[mT, nrmT], W=[coefA])
            em.op("dve", lambda e, o=o, w=w, mv=mv: e.tensor_copy(
                out=coefB[:, o:o + KT], in_=mv[:, sh0:sh0 + KT, w]), R=[mT], W=[coefB])

    coefs(0, modT[0], 0, 16)
    coefs(1, modT[0], 48, 64)
    coefs(2, kvmodT, 0, 16)
    coefs(3, modT[1], 0, 16)
    coefs(4, modT[1], 48, 64)

    class _Rep:
        name = "tmB"

        def __getitem__(self, idx):
            return tmB[:].rearrange("p (k m) -> p k m", m=128)[idx]
    rep_h = _Rep()

    def gate_rep(l, which, gi, dst):
        rep = tmB[:].rearrange("p (k m) -> p k m", m=128)
        em.op("dve", lambda e: e.tensor_copy(
            out=rep, in_=scT[:].rearrange("p (k w) -> p k w", w=2)[:, :, which:which + 1].to_broadcast([128, KT, 128])),
            R=[scT], W=[tmB])
        em.dma("sp", tmC[:], mod_b[l:l + 1, gi * D:(gi + 1) * D].partition_broadcast(128), R=[], W=[tmC])

        def cons(p, n0, nb):
            em.op("dve", lambda e: e.tensor_tensor(out=dst[:, n0:n0 + nb], in0=p[:, 0:nb], in1=tmC[:, n0:n0 + nb],
                                                   op=ALU.add), R=[p, tmC], W=[dst])
        linear(mod_w[l][:, gi * D:(gi + 1) * D], D, cons, lhs=rep_h)

    def peer(l, which, kind):
        norm_mod_T(kind, which)

        def cons_q(p, n0, nb):
            em.op("act", lambda e: e.copy(out=z[:, n0:n0 + nb], in_=p[:, 0:nb]), R=[p], W=[z])
        linear(peer_w_q[l], D, cons_q)
        transpose_back(tmA, hT)
        transpose_to(hT, z, r32=False)
        for hp in range(16):
            pcol = hp % 2
            pw = psW[hp // 4]
            em.op("pe", lambda e, hp=hp, pcol=pcol, pw=pw: e.matmul(
                pw[:, (hp % 4) * 128:(hp % 4 + 1) * 128], lhsT=hT32[:, hp, :], rhs=skT[l][:, pcol * 128:(pcol + 1) * 128],
                start=True, stop=True), R=[hT, skT[l]], W=[pw])
        for q4 in range(4):
            em.op("act", lambda e, q4=q4: e.copy(out=tmB[:, q4 * 512:(q4 + 1) * 512], in_=psW[q4][:]),
                  R=[psW[q4]], W=[tmB])
        for hp in range(16):
            sv = tmB[:, hp * 128:(hp + 1) * 128]
            em.op("dve", lambda e, hp=hp, sv=sv: e.max(out=top[:, hp, 0:8], in_=sv), R=[tmB], W=[top])
            em.op("dve", lambda e, hp=hp, sv=sv: e.max_index(out=tidx[:, hp, 0:8], in_max=top[:, hp, 0:8], in_values=sv),
                  R=[tmB, top], W=[tidx])
            em.op("dve", lambda e, hp=hp, sv=sv: e.match_replace(out=mr[:, 0:128], in_to_replace=top[:, hp, 0:8],
                                                                in_values=sv, imm_value=NEG), R=[tmB, top], W=[mr])
            em.op("dve", lambda e, hp=hp: e.max(out=top[:, hp, 8:16], in_=mr[:, 0:128]), R=[mr], W=[top])
            em.op("dve", lambda e, hp=hp: e.max_index(out=tidx[:, hp, 8:16], in_max=top[:, hp, 8:16],
                                                      in_values=mr[:, 0:128]), R=[mr, top], W=[tidx])
        em.op("dve", lambda e: e.tensor_copy(out=tidxf[:], in_=tidx[:]), R=[tidx], W=[tidxf])
        oh = z[:].rearrange("p (k c) -> p k c", c=256)
        for h in range(8):
            c3 = cand[:, h, :].rearrange("p (a b) -> p a b", b=16)
            i3 = cidx[:, h, :].rearrange("p (a b) -> p a b", b=16)
            em.op("dve", lambda e, h=h, c3=c3: e.tensor_tensor(
                out=c3, in0=top[:, 2 * h, :].unsqueeze(2).to_broadcast([128, 16, 16]),
                in1=top[:, 2 * h + 1, :].unsqueeze(1).to_broadcast([128, 16, 16]), op=ALU.add), R=[top], W=[cand])
            em.op("dve", lambda e, h=h, i3=i3: e.scalar_tensor_tensor(
                out=i3, in0=tidxf[:, 2 * h, :].unsqueeze(2).to_broadcast([128, 16, 16]), scalar=128.0,
                in1=tidxf[:, 2 * h + 1, :].unsqueeze(1).to_broadcast([128, 16, 16]), op0=ALU.mult, op1=ALU.add),
                R=[tidxf], W=[cidx])
            em.op("dve", lambda e, h=h: e.max(out=tops[:, h, 0:8], in_=cand[:, h, :]), R=[cand], W=[tops])
            em.op("dve", lambda e, h=h: e.match_replace(out=mr[:], in_to_replace=tops[:, h, 0:8], in_values=cand[:, h, :],
                                                        imm_value=NEG), R=[cand, tops], W=[mr])
            em.op("dve", lambda e, h=h: e.max(out=tops[:, h, 8:16], in_=mr[:]), R=[mr], W=[tops])
            em.op("dve", lambda e, h=h: e.tensor_tensor(
                out=oh, in0=cand[:, h, :].unsqueeze(1).to_broadcast([128, 16, 256]),
                in1=tops[:, h, :].unsqueeze(2).to_broadcast([128, 16, 256]), op=ALU.is_equal), R=[cand, tops], W=[z])
            em.op("dve", lambda e, h=h: e.tensor_tensor(
                out=oh, in0=oh, in1=cidx[:, h, :].unsqueeze(1).to_broadcast([128, 16, 256]), op=ALU.mult),
                R=[z, cidx], W=[z])
            em.op("dve", lambda e, h=h: e.tensor_reduce(out=eidf[:, h * 16:(h + 1) * 16], in_=oh, axis=AX.X, op=ALU.add),
                  R=[z], W=[eidf])
            em.op("dve", lambda e, h=h: e.tensor_scalar(out=small[:, 8 + h:9 + h], in0=tops[:, h, 0:1], scalar1=-1.0,
                                                        scalar2=None, op0=ALU.mult), R=[tops], W=[small])
            em.op("dve", lambda e, h=h: e.memset(zsum[:, h:h + 1], 0.0), W=[zsum])
            em.op("act", lambda e, h=h: e.activation(out=gate[:, h * 16:(h + 1) * 16], in_=tops[:, h, :], func=AF.Exp,
                                                     bias=small[:, 8 + h:9 + h], scale=1.0,
                                                     accum_out=zsum[:, h:h + 1]), R=[tops, small], W=[gate, zsum])
        em.op("dve", lambda e: e.reciprocal(out=zsum[:, 8:16], in_=zsum[:, 0:8]), R=[zsum], W=[zsum])
        em.op("dve", lambda e: e.tensor_tensor(
            out=gate[:].rearrange("p (h k) -> p h k", k=16), in0=gate[:].rearrange("p (h k) -> p h k", k=16),
            in1=zsum[:, 8:16].unsqueeze(2).to_broadcast([128, 8, 16]), op=ALU.mult), R=[gate, zsum], W=[gate])
        em.op("dve", lambda e: e.tensor_scalar(out=eidf[:], in0=eidf[:], scalar1=0.0, scalar2=16383.0,
                                               op0=ALU.max, op1=ALU.min), R=[eidf], W=[eidf])
        if l > 0:
            em.op("dve", lambda e: e.tensor_scalar(out=eidf[:], in0=eidf[:], scalar1=float(l * 16384), scalar2=None,
                                                   op0=ALU.add), R=[eidf], W=[eidf])
        em.op("dve", lambda e: e.tensor_copy(out=eid[:], in_=eidf[:]), R=[eidf], W=[eid])
        em.op("dve", lambda e: e.memset(actv[:], 0.0), W=[actv])
        for j in range(128):
            g = gb[j % 4]
            em.dma("pool", g[:], peer_u, R=[eid], W=[g], fn=lambda e, g=g, j=j: e.indirect_dma_start(
                out=g[:], out_offset=None, in_=peer_u,
                in_offset=bass.IndirectOffsetOnAxis(ap=eid[:, j:j + 1], axis=0)))
            em.op("dve", lambda e, g=g, j=j: e.scalar_tensor_tensor(
                out=g[:], in0=g[:], scalar=1.0, in1=tmA[:], op0=ALU.mult, op1=ALU.mult,
                accum_out=actv[:, j:j + 1]), R=[g, tmA], W=[g, actv])
        em.op("act", lambda e: e.activation(out=ga[:], in_=actv[:], func=AF.Gelu), R=[actv], W=[ga])
        em.op("dve", lambda e: e.tensor_tensor(out=ga[:], in0=ga[:], in1=gate[:], op=ALU.mult), R=[ga, gate], W=[ga])
        for j in range(128):
            g = gb[j % 4]
            em.dma("pool", g[:], peer_v, R=[eid], W=[g], fn=lambda e, g=g, j=j: e.indirect_dma_start(
                out=g[:], out_offset=None, in_=peer_v,
                in_offset=bass.IndirectOffsetOnAxis(ap=eid[:, j:j + 1], axis=0)))
            acc = tmC if j % 2 == 0 else tmB
            if j < 2:
                em.op("dve", lambda e, g=g, j=j, acc=acc: e.tensor_scalar(out=acc[:], in0=g[:], scalar1=ga[:, j:j + 1],
                                                                          scalar2=None, op0=ALU.mult), R=[g, ga], W=[acc])
            else:
                em.op("dve", lambda e, g=g, j=j, acc=acc: e.scalar_tensor_tensor(
                    out=acc[:], in0=g[:], scalar=ga[:, j:j + 1], in1=acc[:], op0=ALU.mult, op1=ALU.add),
                    R=[g, ga, acc], W=[acc])
        em.op("dve", lambda e: e.tensor_tensor(out=tmC[:], in0=tmC[:], in1=tmB[:], op=ALU.add), R=[tmC, tmB], W=[tmC])
        em.op("dve", lambda e: e.tensor_tensor(out=tmC[:], in0=tmC[:], in1=grep[1][:], op=ALU.mult),
              R=[tmC, grep[1]], W=[tmC])
        em.op("dve", lambda e: e.tensor_tensor(out=x[:], in0=x[:], in1=tmC[:], op=ALU.add), R=[x, tmC], W=[x])

    def mix_out(wap):
        def cons_o(p, n0, nb):
            em.op("dve", lambda e: e.tensor_tensor(out=tmC[:, n0:n0 + nb], in0=p[:, 0:nb], in1=grep[0][:, n0:n0 + nb],
                                                   op=ALU.mult), R=[p, grep[0]], W=[tmC])
        linear(wap, D, cons_o)
        em.op("dve", lambda e: e.tensor_tensor(out=x[:], in0=x[:], in1=tmC[:], op=ALU.add), R=[x, tmC], W=[x])

    def gmlp(which, t):
        norm_mod_T(0, which)

        def cons_z(p, n0, nb):
            em.op("act", lambda e: e.activation(out=z[:, n0:n0 + nb], in_=p[:, 0:nb], func=AF.Gelu), R=[p], W=[z])
        linear(a_w_in, 2 * D, cons_z, bias_ap=a_b_in)
        rstd_of(z[:, D:2 * D], z, 1, D)
        em.op("dve", lambda e: e.scalar_tensor_tensor(out=tmA[:], in0=z[:, D:2 * D], scalar=small[:, 1:2], in1=vg_rep[:],
                                                      op0=ALU.mult, op1=ALU.mult), R=[z, small, vg_rep], W=[tmA])
        if t == NA - 1:
            em.dma("sp", gv_o[:, :], tmA[:], R=[tmA], W=[gv_o], ckey="out_gv")
        for g in range(16):
            pw = psW[g // 4]
            em.op("pe", lambda e, g=g, pw=pw: e.matmul(pw[:, (g % 4) * 128:(g % 4 + 1) * 128], lhsT=wsT[:, g, :],
                                                       rhs=tmA[:, g * 128:(g + 1) * 128], start=True, stop=True),
                  R=[wsT, tmA], W=[pw])
        for g in range(16):
            pw = psW[g // 4]
            em.op("dve", lambda e, g=g, pw=pw: e.scalar_tensor_tensor(
                out=tmB[:, g * 128:(g + 1) * 128], in0=pw[:, (g % 4) * 128:(g % 4 + 1) * 128], scalar=bsT[:, g:g + 1],
                in1=z[:, g * 128:(g + 1) * 128], op0=ALU.add, op1=ALU.mult), R=[pw, bsT, z], W=[tmB])
        transpose_to(hT, tmB)
        mix_out(a_w_out)

    def shared_kv(which, t):
        norm_mod_T(2, which)

        def cons_kv(p, n0, nb):
            em.op("act", lambda e: e.copy(out=z[:, n0:n0 + nb], in_=p[:, 0:nb]), R=[p], W=[z])
        linear(kv_w, 2 * D, cons_kv)
        rows = slice(t * 128, (t + 1) * 128)
        em.dma("sp", v_o[rows, :], z[:, D:2 * D], R=[z], W=[v_o], ckey="out_v")
        em.dma("sp", kv_all[rows, D:2 * D], z[:, D:2 * D], R=[z], W=["kv_all"], ckey="kv_all_w")
        head_norm(z[:, 0:D], z, kng, tmA)
        em.dma("sp", k_o[rows, :], tmA[:], R=[tmA], W=[k_o], ckey="out_k")
        em.dma("sp", kv_all[rows, 0:D], tmA[:], R=[tmA], W=["kv_all"], ckey="kv_all_w")

    def sb_block(hg, HG, nq, Kb, Vb, mask, first, last, s, o_fn):
        W = HG * nq
        base = "wsTs" if s == 0 else "vg_rep"
        bt = wsT[:].rearrange("p g t -> p (g t)") if s == 0 else vg_rep[:]
        ez, sp_, lm, a_ = (bt[:, i * 512:i * 512 + W] for i in range(4))
        kez, ksp, klm, ka = (base + "#" + n for n in ("ez", "sp", "lm", "a"))
        tt = att_t[:, s * 512:s * 512 + W]
        ktt = "att_t#%d" % s
        if HG == 4:
            ktk = "cand#kt%d" % s
            ktv = cand[:].rearrange("p h c -> p (h c)")[:, s * 512:(s + 1) * 512].rearrange("p (h k) -> p h k", k=128)
        else:
            ktk = "cand"
            ktv = cand[:].rearrange("p h c -> p (h c)").rearrange("p (h k) -> p h k", k=128)
        transpose_blocks(HG, lambda b: Kb[:, b * 128:(b + 1) * 128], [Kb],
                         lambda g0, n: ktv[:, g0:g0 + n, :], ktk)
        zp = psL[st["pl"]]
        lp = psL[st["pl"] ^ 1] if HG == 16 else psW[1 + s]
        if HG != 16:
            st["pl"] ^= 1
        for i in range(HG):
            em.op("pe", lambda e, i=i: e.matmul(zp[:, i * nq:(i + 1) * nq], lhsT=ktv[:, i, :],
                                                rhs=hT32[:, hg * HG + i, 0:nq], start=True, stop=False),
                  R=[ktk, hT], W=[zp])
        brow = biasg[0:1, 0:W] if HG == 4 else biass[0:1, 0:W]
        em.op("pe", lambda e: e.matmul(zp[:, 0:W], lhsT=ones_row[0:1, :], rhs=brow, start=False, stop=True),
              R=[ones_row, biasg, biass], W=[zp])
        em.op("act", lambda e: e.activation(out=ez, in_=zp[:, 0:W], func=AF.Exp), R=[zp], W=[kez])
        em.op("act", lambda e: e.activation(out=sp_, in_=ez, func=AF.Ln, bias=1.0, scale=1.0), R=[kez], W=[ksp])
        v3 = lambda ap: ap.rearrange("p (h q) -> p h q", q=nq)
        if mask is None:
            em.op("dve", lambda e: e.tensor_scalar(out=lm, in0=sp_, scalar1=-1.0, scalar2=None, op0=ALU.mult),
                  R=[ksp], W=[klm])
        else:
            em.op("dve", lambda e: e.scalar_tensor_tensor(out=v3(lm), in0=v3(sp_), scalar=-1.0, in1=mask,
                                                          op0=ALU.mult, op1=ALU.mult), R=[ksp, amask, smask], W=[klm])
        em.op("pe", lambda e: e.matmul(lp[:, 0:W], lhsT=umat[:], rhs=lm, start=True, stop=first),
              R=[umat, klm], W=[lp])
        if not first:
            em.op("pe", lambda e: e.matmul(lp[:, 0:W], lhsT=ones128[:], rhs=lacc[:, 0:W], start=False, stop=True),
                  R=[ones128, lacc], W=[lp])
        em.op("dve", lambda e: e.tensor_tensor(out=tt, in0=zp[:, 0:W], in1=sp_, op=ALU.subtract), R=[zp, ksp], W=[ktt])
        em.op("dve", lambda e: e.tensor_tensor(out=tt, in0=lp[:, 0:W], in1=tt, op=ALU.add), R=[lp, ktt], W=[ktt])
        em.op("act", lambda e: e.activation(out=a_, in_=tt, func=AF.Exp), R=[ktt], W=[ka])
        if mask is not None:
            em.op("dve", lambda e: e.tensor_tensor(out=v3(a_), in0=v3(a_), in1=mask, op=ALU.mult),
                  R=[ka, amask, smask], W=[ka])
        if not last:
            if first:
                em.op("dve", lambda e: e.tensor_copy(out=lacc[:, 0:W], in_=lm), R=[klm], W=[lacc])
            else:
                em.op("dve", lambda e: e.tensor_tensor(out=lacc[:, 0:W], in0=lacc[:, 0:W], in1=lm, op=ALU.add),
                      R=[lacc, klm], W=[lacc])
        for i in range(HG):
            pt_, oap = o_fn(i)
            em.op("pe", lambda e, i=i, oap=oap: e.matmul(oap, lhsT=a_[:, i * nq:(i + 1) * nq],
                                                         rhs=Vb[:, i * 128:(i + 1) * 128], start=first, stop=last),
                  R=[ka, Vb], W=[pt_])

    def attn_prompt(j):
        nk = 4 * j + 4
        it = 0
        for hg in range(4):
            em.dma("sp", biasg[:], biasp_d[0:1, hg * 512:(hg + 1) * 512], R=[], W=[biasg])
            for n, kc in enumerate(range(nk - 1, -1, -1)):
                s = it % 2
                it += 1
                Kb, Vb = gb[s], gb[2 + s]
                rows = slice(kc * 128, (kc + 1) * 128)
                em.dma("sp", Kb[:, 0:512], kv_all[rows, hg * 512:(hg + 1) * 512], R=["kv_all"], W=[Kb])
                em.dma("sp", Vb[:, 0:512], kv_all[rows, D + hg * 512:D + (hg + 1) * 512], R=["kv_all"], W=[Vb])
                d = kc - 4 * j
                mask = None if d < 0 else amask[:, d * 128:(d + 1) * 128].unsqueeze(1).to_broadcast([128, 4, 128])
                sb_block(hg, 4, 128, Kb, Vb, mask, n == 0, n == nk - 1, s,
                         lambda i: (psW[0], psW[0][:, i * 128:(i + 1) * 128]))
            em.op("act", lambda e, hg=hg: e.copy(out=tmB[:, hg * 512:(hg + 1) * 512], in_=psW[0][:]),
                  R=[psW[0]], W=[tmB])

    def attn_sample():
        blocks = [None] + list(range(127, -1, -1))
        for n, pg in enumerate(blocks):
            s = n % 2
            Kb, Vb = gb[s], gb[2 + s]
            if pg is None:
                rows = slice((NA - 1) * 128, NA * 128)
                em.dma("sp", Kb[:], kv_all[rows, 0:D], R=["kv_all"], W=[Kb])
                em.dma("sp", Vb[:], kv_all[rows, D:2 * D], R=["kv_all"], W=[Vb])
                mask = smask[:].rearrange("p (h q) -> p h q", q=4)
            else:
                for buf, src in ((Kb, cache_k), (Vb, cache_v)):
                    em.dma("pool", buf[:], src, R=[pidx], W=[buf], fn=lambda e, buf=buf, src=src, pg=pg:
                           e.indirect_dma_start(out=buf[:], out_offset=None, in_=src,
                                                in_offset=bass.IndirectOffsetOnAxis(ap=pidx[:, pg:pg + 1], axis=0)))
                mask = None
            sb_block(0, 16, 4, Kb, Vb, mask, n == 0, n == len(blocks) - 1, s,
                     lambda i: (psW[i // 4], psW[i // 4][0:4, (i % 4) * 128:(i % 4 + 1) * 128]))
        em.op("dve", lambda e: e.memset(tmB[:], 0.0), W=[tmB])
        for q4 in range(4):
            em.op("act", lambda e, q4=q4: e.copy(out=tmB[0:4, q4 * 512:(q4 + 1) * 512], in_=psW[q4][0:4, :]),
                  R=[psW[q4]], W=[tmB])

    def layer1(which, attn):
        norm_mod_T(3, which)

        def cons_q(p, n0, nb):
            em.op("act", lambda e: e.copy(out=z[:, n0:n0 + nb], in_=p[:, 0:nb]), R=[p], W=[z])
        linear(b_w_q, D, cons_q)
        head_norm(z[:, 0:D], z, qng, tmA, extra=128.0 ** -0.5)
        transpose_to(hT, tmA, r32=False)
        attn()
        transpose_to(hT, tmB)
        mix_out(b_w_o)
        peer(1, which, 4)

    for which, tiles in ((0, range(0, NP)), (1, range(NP, NA))):
        gate_rep(0, which, 2, grep[0])
        gate_rep(0, which, 5, grep[1])
        for t in tiles:
            src = xall[t * 128:(t + 1) * 128, :] if t < NP else xsmp
            em.dma("sp", x[:], src, R=[], W=[x])
            gmlp(which, t)
            peer(0, which, 1)
            shared_kv(which, t)
            em.dma("sp", xs_all[t * 128:(t + 1) * 128, :], x[:], R=[x], W=["xs_all"], ckey="xs_all_w")
    for which, tiles in ((0, range(0, 8)), (1, range(8, 9))):
        gate_rep(1, which, 2, grep[0])
        gate_rep(1, which, 5, grep[1])
        for t in tiles:
            if t < 8:
                em.dma("pool", x[:], xs_all, R=["xs_all", myrows], W=[x], fn=lambda e, t=t: e.indirect_dma_start(
                    out=x[:], out_offset=None, in_=xs_all,
                    in_offset=bass.IndirectOffsetOnAxis(ap=myrows[:, t:t + 1], axis=0)))
                layer1(which, lambda t=t: attn_prompt(t))
            else:
                em.dma("sp", x[:], xs_all[(NA - 1) * 128:NA * 128, :], R=["xs_all"], W=[x])
                layer1(which, attn_sample)
            em.dma("sp", y_o[t * 128:(t + 1) * 128, :], x[:], R=[x], W=[y_o], ckey="out_y")

    em.wait_all_dma("sp")
    em.replay()
    return nc, es


def _fm(vec):
    return np.ascontiguousarray(vec.reshape(-1, 128).T)


def kernel(x_prompt, x_sample, cache_k, cache_v, page_table, c_prompt, c_sample,
           mod_w, mod_b, norm_mix_g, norm_ffn_g,
           a_w_in, a_b_in, a_v_norm_g, a_w_s, a_b_s, a_w_out,
           kv_mod_w, kv_mod_b, kv_norm_g, kv_w, k_norm_g,
           b_w_q, b_q_norm_g, b_logit_bias, b_w_o,
           peer_w_q, peer_subkeys, peer_u, peer_v):
    f = lambda a: np.ascontiguousarray(np.asarray(a, dtype=np.float32))
    x_prompt, x_sample, c_prompt, c_sample = f(x_prompt), f(x_sample), f(c_prompt), f(c_sample)
    mod_w, mod_b = f(mod_w), f(mod_b)
    lb = f(b_logit_bias)[0]
    kq = np.arange(128)
    shared = {
        "mod_w": mod_w, "mod_b": mod_b,
        "mod_bT": np.stack([_fm(mod_b[l]) for l in range(2)]),
        "normT": np.concatenate([_fm(f(norm_mix_g)[0]), _fm(f(norm_ffn_g)[0]), _fm(f(kv_norm_g)),
                                 _fm(f(norm_mix_g)[1]), _fm(f(norm_ffn_g)[1])], axis=1),
        "a_w_in": f(a_w_in)[0], "a_b_in": f(a_b_in)[0:1], "a_vg": f(a_v_norm_g)[0:1],
        "wsT": np.ascontiguousarray(np.transpose(f(a_w_s)[0], (2, 0, 1)).reshape(128, 16 * 128)),
        "bsT": np.ascontiguousarray(f(a_b_s)[0].T),
        "a_w_out": f(a_w_out)[0],
        "kv_mod_w": f(kv_mod_w), "kv_mod_bT": _fm(f(kv_mod_b)), "kv_w": f(kv_w),
        "k_norm_g": f(k_norm_g).reshape(1, 128), "q_norm_g": f(b_q_norm_g)[0].reshape(1, 128),
        "b_w_q": f(b_w_q)[0], "b_w_o": f(b_w_o)[0],
        "peer_w_q": f(peer_w_q),
        "skT": np.ascontiguousarray(np.transpose(f(peer_subkeys), (0, 3, 1, 2)).reshape(2, 128, 256)),
        "peer_u": f(peer_u).reshape(2 * 16384, D), "peer_v": f(peer_v).reshape(2 * 16384, D),
        "cache_k": f(cache_k).reshape(1280 * 128, D), "cache_v": f(cache_v).reshape(1280 * 128, D),
        "ident": np.eye(128, dtype=np.float32),
        "tril": np.triu(np.ones((128, 128), np.float32)),
        "umat": np.tril(np.ones((128, 128), np.float32), -1),
        "ones128": np.ones((128, 128), np.float32),
        "smask": np.ascontiguousarray(np.broadcast_to((kq[:, None] < np.arange(4)[None, :]).astype(np.float32)[:, None, :],
                                                      (128, 16, 4)).reshape(128, 64)),
        "biasrow_p": np.ascontiguousarray(np.repeat(lb, 128).reshape(1, D)),
        "biasrow_s": np.ascontiguousarray(np.repeat(lb, 4).reshape(1, 64)),
        "iota_p": np.arange(128, dtype=np.float32).reshape(128, 1),
    }
    strict = (kq[:, None] < kq[None, :]).astype(np.float32)
    in_maps = []
    for c in range(8):
        b, r = c // 4, c % 4
        xs = np.zeros((128, D), np.float32)
        xs[0:4] = x_sample[c]
        am = np.zeros((128, 4, 128), np.float32)
        for d in range(4):
            am[:, d, :] = 1.0 if d < r else (strict if d == r else 0.0)
        m = dict(shared)
        m["xall"] = x_prompt[b]
        m["xsmp"] = xs
        m["c2T"] = np.ascontiguousarray(np.stack([_fm(c_prompt[b]), _fm(c_sample[c])], axis=2).reshape(128, KT * 2))
        m["pt"] = np.ascontiguousarray(np.asarray(page_table, dtype=np.int32)[c].reshape(1, 128))
        m["myrows"] = np.ascontiguousarray(((4 * np.arange(8)[None, :] + r) * 128 + kq[:, None]).astype(np.int32))
        m["amask"] = np.ascontiguousarray(am.reshape(128, 512))
        in_maps.append(m)

    nc, es = build_program()
    with es:
        res = run_bass_kernel_spmd(nc, in_maps, core_ids=list(range(8)))
    R = res.results

    y_prompt = np.zeros((2, 4096, D), np.float32)
    k_prompt = np.zeros((2, 4096, 16, 128), np.float32)
    v_prompt = np.zeros((2, 4096, 16, 128), np.float32)
    y_sample = np.zeros((8, 4, D), np.float32)
    k_sample = np.zeros((8, 4, 16, 128), np.float32)
    v_sample = np.zeros((8, 4, 16, 128), np.float32)
    gmlp_v = np.zeros((1, 8, 4, D), np.float32)
    for c in range(8):
        b, r = c // 4, c % 4
        y, ko, vo, gv = (np.asarray(R[c][k]) for k in ("y", "ko", "vo", "gv"))
        for j in range(8):
            ch = 4 * j + r
            y_prompt[b, ch * 128:(ch + 1) * 128] = y[j * 128:(j + 1) * 128]
            k_prompt[b, ch * 128:(ch + 1) * 128] = ko[ch * 128:(ch + 1) * 128].reshape(128, 16, 128)
            v_prompt[b, ch * 128:(ch + 1) * 128] = vo[ch * 128:(ch + 1) * 128].reshape(128, 16, 128)
        y_sample[c] = y[1024:1028]
        k_sample[c] = ko[NP * 128:NP * 128 + 4].reshape(4, 16, 128)
        v_sample[c] = vo[NP * 128:NP * 128 + 4].reshape(4, 16, 128)
        gmlp_v[0, c] = gv[0:4]
    return (y_prompt, y_sample, k_prompt, v_prompt, k_sample, v_sample, gmlp_v)
```

```python
from contextlib import ExitStack
import numpy as np
import concourse.bass as bass
import concourse.mybir as mybir
from concourse.bass_utils import run_bass_kernel_spmd

F32 = mybir.dt.float32
I32 = mybir.dt.int32
U32 = mybir.dt.uint32
ALU = mybir.AluOpType
AF = mybir.ActivationFunctionType
AX = mybir.AxisListType

D = 2048
KT = 16
NT = 9
NP = 32
NA = NP + 1
NB = 256
EPS = 1e-6
NEG = -1.0e30
SAME_ENGINE_RAW_SYNC = True


def _key(x):
    if isinstance(x, str):
        return x
    n = getattr(x, "name", None)
    if isinstance(n, str):
        return n
    return x.tensor.name


class Em:
    def __init__(self, nc, es):
        self.nc, self.es = nc, es
        self.engs = ["pe", "act", "dve", "pool", "sp"]
        self.ops = {e: [] for e in self.engs}
        self.psem = {e: es.enter_context(nc.semaphore("prog_" + e)) for e in self.engs}
        self.cnt = {e: 0 for e in self.engs}
        self.waited = {}
        self.bufs = {}
        self.chan = {}
        self.untracked = set()

    def sb(self, name, shape, dt=F32):
        return self.es.enter_context(self.nc.sbuf_tensor(name, list(shape), dt))

    def ps(self, name, shape, dt=F32):
        return self.es.enter_context(self.nc.psum_tensor(name, list(shape), dt))

    def _st(self, k):
        return self.bufs.setdefault(k, {"w": None, "r": []})

    def _tokval(self, tok):
        if tok[0] == "eng":
            return ("p", tok[1]), self.psem[tok[1]], tok[2]
        ch = self.chan[tok[1]]
        return ("c", tok[1]), ch[0], ch[1]

    def _rel(self, k):
        if "#" in k:
            return [k, k.split("#")[0]], [k]
        subs = [q for q in self.bufs if q.startswith(k + "#")]
        return [k] + subs, [k] + subs

    def _deps(self, eng, R, W):
        R = [q for k in R if k not in self.untracked for q in self._rel(k)[0]]
        W = [q for k in W for q in self._rel(k)[0]]
        waits = []

        def need(tok, raw):
            if tok is None:
                return
            if tok[0] == "eng" and tok[1] == eng:
                if not (raw and SAME_ENGINE_RAW_SYNC and eng in ("act", "dve", "pool")):
                    return
            sk, sem, val = self._tokval(tok)
            if self.waited.get((eng, sk), 0) >= val:
                return
            self.waited[(eng, sk)] = val
            waits.append((sem, val))

        for k in R:
            if k in self.untracked:
                continue
            need(self._st(k)["w"], True)
        for k in W:
            st = self._st(k)
            need(st["w"], False)
            for t in st["r"]:
                need(t, False)
        return waits

    def _commit(self, tok, R, W):
        R = [q for k in R if k not in self.untracked for q in self._rel(k)[1]]
        W = [q for k in W for q in self._rel(k)[1]]
        for k in R:
            if k in self.untracked:
                continue
            self._st(k)["r"].append(tok)
        for k in W:
            st = self._st(k)
            st["w"] = tok
            st["r"] = []

    def op(self, eng, fn, R=(), W=()):
        R = [_key(x) for x in R]
        W = [_key(x) for x in W]
        waits = self._deps(eng, R, W)
        self.cnt[eng] += 1
        tok = ("eng", eng, self.cnt[eng])
        self.ops[eng].append((waits, fn, (self.psem[eng], 1)))
        self._commit(tok, R, W)

    def dma(self, q, out, in_, R=None, W=None, fn=None, ckey=None):
        Rk = [_key(x) for x in (R if R is not None else [in_])]
        Wk = [_key(x) for x in (W if W is not None else [out])]
        kind = "sw" if q == "pool" else "hw"
        ck = (ckey or (Wk[0] if Wk[0] not in self.untracked_out else Rk[0]), kind)
        if ck not in self.chan:
            self.chan[ck] = [self.es.enter_context(self.nc.semaphore("ch%d" % len(self.chan))), 0]
        waits = self._deps(q, Rk, Wk)
        ch = self.chan[ck]
        ch[1] += 16
        if fn is None:
            fn = lambda e, o=out, i=in_: e.dma_start(out=o, in_=i)
        self.ops[q].append((waits, fn, (ch[0], 16)))
        self._commit(("dma", ck), Rk, Wk)

    untracked_out = set()

    def wait_all_dma(self, eng="sp"):
        waits = []
        for ck, ch in self.chan.items():
            if ch[1] > 0:
                waits.append((ch[0], ch[1]))
        self.ops[eng].append((waits, None, None))

    def replay(self):
        with self.nc.Block() as blk:
            def run(e, name):
                for waits, fn, inc in self.ops[name]:
                    for sem, val in waits:
                        e.wait_ge(sem, val)
                    if fn is None:
                        continue
                    ins = fn(e)
                    if inc is not None:
                        ins.then_inc(inc[0], inc[1])

            @blk.tensor
            def _(e):
                run(e, "pe")

            @blk.scalar
            def _(e):
                run(e, "act")

            @blk.vector
            def _(e):
                run(e, "dve")

            @blk.gpsimd
            def _(e):
                run(e, "pool")

            @blk.sync
            def _(e):
                run(e, "sp")


def build_program():
    nc = bass.Bass("TRN2", target_bir_lowering=False)
    es = ExitStack()
    em = Em(nc, es)

    def din(name, shape, dt=F32):
        t = nc.dram_tensor(name, list(shape), dt, kind="ExternalInput")
        em.untracked.add(name)
        return t.ap()

    def dout(name, shape, dt=F32):
        return nc.dram_tensor(name, list(shape), dt, kind="ExternalOutput").ap()

    xall = din("xall", [NP * 128, D])
    xsmp = din("xsmp", [128, D])
    c2T = din("c2T", [128, KT * 2])
    pt_d = din("pt", [1, 128], I32)
    myrows_d = din("myrows", [128, 8], I32)
    amask_d = din("amask", [128, 4 * 128])
    smask_d = din("smask", [128, 64])
    umat_d = din("umat", [128, 128])
    ones_d = din("ones128", [128, 128])
    biasp_d = din("biasrow_p", [1, D])
    biass_d = din("biasrow_s", [1, 64])
    iota_d = din("iota_p", [128, 1])
    mod_w = din("mod_w", [2, D, 6 * D])
    mod_bT = din("mod_bT", [2, 128, 96])
    mod_b = din("mod_b", [2, 6 * D])
    normT = din("normT", [128, 5 * KT])
    a_w_in = din("a_w_in", [D, 2 * D])
    a_b_in = din("a_b_in", [1, 2 * D])
    a_vg = din("a_vg", [1, D])
    wsT_d = din("wsT", [128, 16 * 128])
    bsT_d = din("bsT", [128, 16])
    a_w_out = din("a_w_out", [D, D])
    kv_mod_w = din("kv_mod_w", [D, 2 * D])
    kv_mod_bT = din("kv_mod_bT", [128, 32])
    kv_w = din("kv_w", [D, 2 * D])
    kng_d = din("k_norm_g", [1, 128])
    qng_d = din("q_norm_g", [1, 128])
    b_w_q = din("b_w_q", [D, D])
    b_w_o = din("b_w_o", [D, D])
    peer_w_q = din("peer_w_q", [2, D, D])
    skT_d = din("skT", [2, 128, 2 * 128])
    peer_u = din("peer_u", [2 * 16384, D])
    peer_v = din("peer_v", [2 * 16384, D])
    cache_k = din("cache_k", [1280 * 128, D])
    cache_v = din("cache_v", [1280 * 128, D])
    ident_d = din("ident", [128, 128])
    tril_d = din("tril", [128, 128])

    y_o = dout("y", [NT * 128, D])
    k_o = dout("ko", [NA * 128, D])
    v_o = dout("vo", [NA * 128, D])
    gv_o = dout("gv", [128, D])
    kv_all = nc.dram_tensor("kv_all", [NA * 128, 2 * D], F32).ap()
    xs_all = nc.dram_tensor("xs_all", [NA * 128, D], F32).ap()

    x = em.sb("x", [128, D])
    tmA = em.sb("tmA", [128, D])
    tmB = em.sb("tmB", [128, D])
    tmC = em.sb("tmC", [128, D])
    hT = em.sb("hT", [128, KT, 128])
    z = em.sb("z", [128, 2 * D])
    wbuf = [em.sb("wbuf%d" % i, [128, KT + 1, NB]) for i in range(2)]
    wsT = em.sb("wsTs", [128, 16, 128])
    vg_rep = em.sb("vg_rep", [128, D])
    grep = [em.sb("grep%d" % i, [128, D]) for i in range(2)]
    cand = em.sb("cand", [128, 8, 256])
    cidx = em.sb("cidx", [128, 8, 256])
    gb = [em.sb("gb%d" % i, [128, D]) for i in range(4)]
    ident = em.sb("ident_s", [128, 128])
    tril = em.sb("tril_s", [128, 128])
    ones_row = em.sb("ones_row", [1, 128])
    scT = em.sb("scT", [128, KT * 2])
    modT = [em.sb("modT%d" % l, [128, 96 * 2]) for l in range(2)]
    kvmodT = em.sb("kvmodT", [128, 32 * 2])
    mbT = [em.sb("mbT%d" % l, [128, 96]) for l in range(2)]
    kvmbT = em.sb("kvmbT", [128, 32])
    nrmT = em.sb("nrmT", [128, 5 * KT])
    coefA = em.sb("coefA", [128, 5 * 2 * KT])
    coefB = em.sb("coefB", [128, 5 * 2 * KT])
    bsT = em.sb("bsTs", [128, 16])
    kng = em.sb("kng", [128, 128])
    qng = em.sb("qng", [128, 128])
    skT = [em.sb("skT%d" % l, [128, 256]) for l in range(2)]
    small = em.sb("small", [128, 64])
    top = em.sb("top", [128, 16, 16])
    tidx = em.sb("tidx", [128, 16, 16], U32)
    tidxf = em.sb("tidxf", [128, 16, 16])
    mr = em.sb("mr", [128, 256])
    tops = em.sb("tops", [128, 8, 16])
    eidf = em.sb("eidf", [128, 128])
    eid = em.sb("eid", [128, 128], I32)
    gate = em.sb("gate", [128, 128])
    actv = em.sb("actv", [128, 128])
    ga = em.sb("ga", [128, 128])
    zsum = em.sb("zsum", [128, 16])
    att_t = em.sb("att_t", [128, 2 * 512])
    lacc = em.sb("lacc", [128, 512])
    amask = em.sb("amask_s", [128, 4 * 128])
    smask = em.sb("smask_s", [128, 64])
    umat = em.sb("umat_s", [128, 128])
    ones128 = em.sb("ones128_s", [128, 128])
    biasp = em.sb("biasp", [1, D])
    biass = em.sb("biass", [1, 64])
    iota_p = em.sb("iota_ps", [128, 1])
    pti = em.sb("pti", [128, 128], I32)
    ptf = em.sb("ptf", [128, 128])
    pidx = em.sb("pidx", [128, 128], I32)
    myrows = em.sb("myrows_s", [128, 8], I32)

    psL = [em.ps("psL%d" % i, [128, 512]) for i in range(2)]
    psT = [em.ps("psT%d" % i, [128, 512]) for i in range(2)]
    psW = [em.ps("psW%d" % i, [128, 512]) for i in range(4)]

    st = {"wslot": 0, "pl": 0, "pt": 0}

    def load_w(wap, n0, nb, bias_ap=None):
        s = st["wslot"]
        st["wslot"] ^= 1
        wb = wbuf[s]
        em.dma("sp", wb[:, 0:KT, 0:nb], wap[:, n0:n0 + nb].rearrange("(kt p) n -> p kt n", p=128),
               R=[], W=[wb])
        if bias_ap is not None:
            em.dma("sp", wb[0:1, KT, 0:nb], bias_ap[0:1, n0:n0 + nb], R=[], W=[wb])
        return wb

    def linear(wap, ncols, consume, bias_ap=None, lhs=None):
        lhs = lhs if lhs is not None else hT
        nblk = ncols // NB
        nxt = load_w(wap, 0, NB, bias_ap)
        for b in range(nblk):
            wb = nxt
            if b + 1 < nblk:
                nxt = load_w(wap, (b + 1) * NB, NB, bias_ap)
            p = psL[st["pl"]]
            st["pl"] ^= 1
            for kt in range(KT):
                last = (kt == KT - 1) and bias_ap is None
                em.op("pe", lambda e, p=p, wb=wb, kt=kt, last=last: e.matmul(
                    p[:, 0:NB], lhsT=lhs[:, kt, :], rhs=wb[:, kt, 0:NB], start=(kt == 0), stop=last),
                    R=[lhs, wb], W=[p])
            if bias_ap is not None:
                em.op("pe", lambda e, p=p, wb=wb: e.matmul(
                    p[:, 0:NB], lhsT=ones_row[0:1, :], rhs=wb[0:1, KT, 0:NB], start=False, stop=True),
                    R=[ones_row, wb], W=[p])
            consume(p, b * NB, NB)

    def transpose_blocks(nblk, src_fn, srckeys, dst_fn, dstkey, coef=None):
        for g0 in range(0, nblk, 4):
            n = min(4, nblk - g0)
            p = psT[st["pt"]]
            st["pt"] ^= 1
            for i in range(n):
                em.op("pe", lambda e, p=p, i=i, blk=g0 + i: e.transpose(
                    p[:, i * 128:(i + 1) * 128], src_fn(blk), ident[:]), R=list(srckeys) + [ident], W=[p])
            if coef is None:
                em.op("act", lambda e, p=p, g0=g0, n=n: e.copy(
                    out=dst_fn(g0, n), in_=p[:, 0:n * 128].rearrange("p (a b) -> p a b", a=n)),
                    R=[p], W=[dstkey])
            else:
                A, B = coef
                for i in range(n):
                    kt = g0 + i
                    em.op("act", lambda e, p=p, i=i, kt=kt, A=A, B=B: e.activation(
                        out=dst_fn(kt, 1)[:, 0, :], in_=p[:, i * 128:(i + 1) * 128], func=AF.Identity,
                        bias=B[:, kt:kt + 1], scale=A[:, kt:kt + 1]),
                        R=[p, coefA, coefB], W=[dstkey])

    def transpose_to(dst, src, coef=None):
        transpose_blocks(KT, lambda b: src[:, b * 128:(b + 1) * 128], [src],
                         lambda g0, n: dst[:, g0:g0 + n, :], dst, coef)

    def transpose_back(dst, srcT):
        transpose_blocks(KT, lambda b: srcT[:, b, :], [srcT],
                         lambda g0, n: dst[:, g0 * 128:(g0 + n) * 128].rearrange("p (a b) -> p a b", a=n), dst)

    def rsqrt_cols(c0, c1, n):
        em.op("dve", lambda e: e.tensor_scalar(out=small[:, c0:c1], in0=small[:, c0:c1],
                                               scalar1=1.0 / n, scalar2=EPS, op0=ALU.mult, op1=ALU.add),
              R=[small], W=[small])
        em.op("act", lambda e: e.activation(out=small[:, c0:c1], in_=small[:, c0:c1], func=AF.Sqrt),
              R=[small], W=[small])
        em.op("dve", lambda e: e.reciprocal(out=small[:, c0:c1], in_=small[:, c0:c1]), R=[small], W=[small])

    def rstd_of(src, srckey, col, n):
        em.op("dve", lambda e: e.memset(small[:, col:col + 1], 0.0), W=[small])
        em.op("act", lambda e: e.activation(out=tmC[:, 0:n], in_=src, func=AF.Square,
                                             accum_out=small[:, col:col + 1]), R=[srckey], W=[tmC, small])
        rsqrt_cols(col, col + 1, n)

    def norm_mod_T(kind, which):
        rstd_of(x[:], x, 0, D)
        em.op("dve", lambda e: e.tensor_scalar(out=tmA[:], in0=x[:], scalar1=small[:, 0:1], scalar2=None,
                                               op0=ALU.mult), R=[x, small], W=[tmA])
        o = (kind * 2 + which) * KT
        transpose_to(hT, tmA, coef=(coefA[:, o:o + KT], coefB[:, o:o + KT]))

    def head_norm(src, srckey, gain, dst, extra=1.0):
        d3 = dst[:].rearrange("p (h d) -> p h d", d=128)
        em.op("dve", lambda e: e.tensor_tensor(out=dst[:], in0=src, in1=src, op=ALU.mult), R=[srckey], W=[dst])
        em.op("dve", lambda e: e.tensor_reduce(out=small[:, 16:32], in_=d3, axis=AX.X, op=ALU.add), R=[dst], W=[small])
        rsqrt_cols(16, 32, 128)
        if extra != 1.0:
            em.op("dve", lambda e: e.tensor_scalar(out=small[:, 16:32], in0=small[:, 16:32], scalar1=extra, scalar2=None,
                                                   op0=ALU.mult), R=[small], W=[small])
        em.op("dve", lambda e: e.tensor_tensor(out=d3, in0=src.rearrange("p (h d) -> p h d", d=128),
                                               in1=small[:, 16:32].unsqueeze(2).to_broadcast([128, 16, 128]),
                                               op=ALU.mult), R=[srckey, small], W=[dst])
        em.op("dve", lambda e: e.tensor_tensor(out=d3, in0=d3, in1=gain[:].unsqueeze(1).to_broadcast([128, 16, 128]),
                                               op=ALU.mult), R=[dst, gain], W=[dst])

    for dst, src in ((ident, ident_d), (tril, tril_d), (scT, c2T), (nrmT, normT), (bsT, bsT_d),
                     (kvmbT, kv_mod_bT), (mbT[0], mod_bT[0]), (mbT[1], mod_bT[1]),
                     (skT[0], skT_d[0]), (skT[1], skT_d[1]), (amask, amask_d), (smask, smask_d),
                     (umat, umat_d), (ones128, ones_d), (biasp, biasp_d), (biass, biass_d),
                     (iota_p, iota_d), (myrows, myrows_d)):
        em.dma("sp", dst[:], src, R=[], W=[dst])
    em.dma("sp", wsT[:].rearrange("p g t -> p (g t)"), wsT_d, R=[], W=[wsT])
    em.dma("sp", vg_rep[:], a_vg.partition_broadcast(128), R=[], W=[vg_rep])
    em.dma("sp", kng[:], kng_d.partition_broadcast(128), R=[], W=[kng])
    em.dma("sp", qng[:], qng_d.partition_broadcast(128), R=[], W=[qng])
    em.dma("sp", pti[:], pt_d.partition_broadcast(128), R=[], W=[pti])
    em.op("dve", lambda e: e.memset(ones_row[:], 1.0), W=[ones_row])
    em.op("dve", lambda e: e.tensor_copy(out=ptf[:], in_=pti[:]), R=[pti], W=[ptf])
    em.op("dve", lambda e: e.tensor_scalar(out=pidx[:], in0=ptf[:], scalar1=128.0, scalar2=iota_p[:, 0:1],
                                           op0=ALU.mult, op1=ALU.add), R=[ptf, iota_p], W=[pidx])
    em.op("dve", lambda e: e.tensor_tensor(out=wsT[:], in0=wsT[:],
                                           in1=tril[:].unsqueeze(1).to_broadcast([128, 16, 128]), op=ALU.mult),
          R=[wsT, tril], W=[wsT])
    em.op("act", lambda e: e.activation(out=scT[:], in_=scT[:], func=AF.Silu), R=[scT], W=[scT])

    def mod_T(wap, ntiles, outT, biasT):
        pm = psW[0]
        nblk = ntiles * 128 // NB
        nxt = load_w(wap, 0, NB)
        for b in range(nblk):
            wb = nxt
            if b + 1 < nblk:
                nxt = load_w(wap, (b + 1) * NB, NB)
            for i in range(NB // 128):
                nt = b * (NB // 128) + i
                for kt in range(KT):
                    em.op("pe", lambda e, wb=wb, i=i, nt=nt, kt=kt: e.matmul(
                        pm[:, nt * 2:nt * 2 + 2], lhsT=wb[:, kt, i * 128:(i + 1) * 128],
                        rhs=scT[:, kt * 2:kt * 2 + 2], start=(kt == 0), stop=(kt == KT - 1)),
                        R=[wb, scT], W=[pm])
        em.op("dve", lambda e: e.tensor_tensor(
            out=outT[:].rearrange("p (n w) -> p n w", w=2), in0=pm[:, 0:ntiles * 2].rearrange("p (n w) -> p n w", w=2),
            in1=biasT[:].unsqueeze(2).to_broadcast([128, ntiles, 2]), op=ALU.add), R=[pm, biasT], W=[outT])

    mod_T(mod_w[0], 96, modT[0], mbT[0])
    mod_T(kv_mod_w, 32, kvmodT, kvmbT)
    mod_T(mod_w[1], 96, modT[1], mbT[1])

    def coefs(kind, mT, sh0, sc0):
        for w in range(2):
            o = (kind * 2 + w) * KT
            mv = mT[:].rearrange("p (n w) -> p n w", w=2)
            em.op("dve", lambda e, o=o, w=w, mv=mv: e.scalar_tensor_tensor(
                out=coefA[:, o:o + KT], in0=mv[:, sc0:sc0 + KT, w], scalar=1.0,
                in1=nrmT[:, kind * KT:(kind + 1) * KT], op0=ALU.add, op1=ALU.mult),
                R=[mT, nrmT], W=[coefA])
            em.op("dve", lambda e, o=o, w=w, mv=mv: e.tensor_copy(
                out=coefB[:, o:o + KT], in_=mv[:, sh0:sh0 + KT, w]), R=[mT], W=[coefB])

    coefs(0, modT[0], 0, 16)
    coefs(1, modT[0], 48, 64)
    coefs(2, kvmodT, 0, 16)
    coefs(3, modT[1], 0, 16)
    coefs(4, modT[1], 48, 64)

    class _Rep:
        name = "tmB"

        def __getitem__(self, idx):
            return tmB[:].rearrange("p (k m) -> p k m", m=128)[idx]
    rep_h = _Rep()

    def gate_rep(l, which, gi, dst):
        rep = tmB[:].rearrange("p (k m) -> p k m", m=128)
        em.op("dve", lambda e: e.tensor_copy(
            out=rep, in_=scT[:].rearrange("p (k w) -> p k w", w=2)[:, :, which:which + 1].to_broadcast([128, KT, 128])),
            R=[scT], W=[tmB])
        em.dma("sp", tmC[:], mod_b[l:l + 1, gi * D:(gi + 1) * D].partition_broadcast(128), R=[], W=[tmC])

        def cons(p, n0, nb):
            em.op("dve", lambda e: e.tensor_tensor(out=dst[:, n0:n0 + nb], in0=p[:, 0:nb], in1=tmC[:, n0:n0 + nb],
                                                   op=ALU.add), R=[p, tmC], W=[dst])
        linear(mod_w[l][:, gi * D:(gi + 1) * D], D, cons, lhs=rep_h)

    def peer(l, which, kind):
        norm_mod_T(kind, which)

        def cons_q(p, n0, nb):
            em.op("act", lambda e: e.copy(out=z[:, n0:n0 + nb], in_=p[:, 0:nb]), R=[p], W=[z])
        linear(peer_w_q[l], D, cons_q)
        transpose_back(tmA, hT)
        transpose_to(hT, z)
        for hp in range(16):
            pcol = hp % 2
            pw = psW[hp // 4]
            em.op("pe", lambda e, hp=hp, pcol=pcol, pw=pw: e.matmul(
                pw[:, (hp % 4) * 128:(hp % 4 + 1) * 128], lhsT=hT[:, hp, :], rhs=skT[l][:, pcol * 128:(pcol + 1) * 128],
                start=True, stop=True), R=[hT, skT[l]], W=[pw])
        for q4 in range(4):
            em.op("act", lambda e, q4=q4: e.copy(out=tmB[:, q4 * 512:(q4 + 1) * 512], in_=psW[q4][:]),
                  R=[psW[q4]], W=[tmB])
        for hp in range(16):
            sv = tmB[:, hp * 128:(hp + 1) * 128]
            em.op("dve", lambda e, hp=hp, sv=sv: e.max(out=top[:, hp, 0:8], in_=sv), R=[tmB], W=[top])
            em.op("dve", lambda e, hp=hp, sv=sv: e.max_index(out=tidx[:, hp, 0:8], in_max=top[:, hp, 0:8], in_values=sv),
                  R=[tmB, top], W=[tidx])
            em.op("dve", lambda e, hp=hp, sv=sv: e.match_replace(out=mr[:, 0:128], in_to_replace=top[:, hp, 0:8],
                                                                in_values=sv, imm_value=NEG), R=[tmB, top], W=[mr])
            em.op("dve", lambda e, hp=hp: e.max(out=top[:, hp, 8:16], in_=mr[:, 0:128]), R=[mr], W=[top])
            em.op("dve", lambda e, hp=hp: e.max_index(out=tidx[:, hp, 8:16], in_max=top[:, hp, 8:16],
                                                      in_values=mr[:, 0:128]), R=[mr, top], W=[tidx])
        em.op("dve", lambda e: e.tensor_copy(out=tidxf[:], in_=tidx[:]), R=[tidx], W=[tidxf])
        oh = z[:].rearrange("p (k c) -> p k c", c=256)
        for h in range(8):
            c3 = cand[:, h, :].rearrange("p (a b) -> p a b", b=16)
            i3 = cidx[:, h, :].rearrange("p (a b) -> p a b", b=16)
            em.op("dve", lambda e, h=h, c3=c3: e.tensor_tensor(
                out=c3, in0=top[:, 2 * h, :].unsqueeze(2).to_broadcast([128, 16, 16]),
                in1=top[:, 2 * h + 1, :].unsqueeze(1).to_broadcast([128, 16, 16]), op=ALU.add), R=[top], W=[cand])
            em.op("dve", lambda e, h=h, i3=i3: e.scalar_tensor_tensor(
                out=i3, in0=tidxf[:, 2 * h, :].unsqueeze(2).to_broadcast([128, 16, 16]), scalar=128.0,
                in1=tidxf[:, 2 * h + 1, :].unsqueeze(1).to_broadcast([128, 16, 16]), op0=ALU.mult, op1=ALU.add),
                R=[tidxf], W=[cidx])
            em.op("dve", lambda e, h=h: e.max(out=tops[:, h, 0:8], in_=cand[:, h, :]), R=[cand], W=[tops])
            em.op("dve", lambda e, h=h: e.match_replace(out=mr[:], in_to_replace=tops[:, h, 0:8], in_values=cand[:, h, :],
                                                        imm_value=NEG), R=[cand, tops], W=[mr])
            em.op("dve", lambda e, h=h: e.max(out=tops[:, h, 8:16], in_=mr[:]), R=[mr], W=[tops])
            em.op("dve", lambda e, h=h: e.tensor_tensor(
                out=oh, in0=cand[:, h, :].unsqueeze(1).to_broadcast([128, 16, 256]),
                in1=tops[:, h, :].unsqueeze(2).to_broadcast([128, 16, 256]), op=ALU.is_equal), R=[cand, tops], W=[z])
            em.op("dve", lambda e, h=h: e.tensor_tensor(
                out=oh, in0=oh, in1=cidx[:, h, :].unsqueeze(1).to_broadcast([128, 16, 256]), op=ALU.mult),
                R=[z, cidx], W=[z])
            em.op("dve", lambda e, h=h: e.tensor_reduce(out=eidf[:, h * 16:(h + 1) * 16], in_=oh, axis=AX.X, op=ALU.add),
                  R=[z], W=[eidf])
            em.op("dve", lambda e, h=h: e.tensor_scalar(out=small[:, 8 + h:9 + h], in0=tops[:, h, 0:1], scalar1=-1.0,
                                                        scalar2=None, op0=ALU.mult), R=[tops], W=[small])
            em.op("dve", lambda e, h=h: e.memset(zsum[:, h:h + 1], 0.0), W=[zsum])
            em.op("act", lambda e, h=h: e.activation(out=gate[:, h * 16:(h + 1) * 16], in_=tops[:, h, :], func=AF.Exp,
                                                     bias=small[:, 8 + h:9 + h], scale=1.0,
                                                     accum_out=zsum[:, h:h + 1]), R=[tops, small], W=[gate, zsum])
        em.op("dve", lambda e: e.reciprocal(out=zsum[:, 8:16], in_=zsum[:, 0:8]), R=[zsum], W=[zsum])
        em.op("dve", lambda e: e.tensor_tensor(
            out=gate[:].rearrange("p (h k) -> p h k", k=16), in0=gate[:].rearrange("p (h k) -> p h k", k=16),
            in1=zsum[:, 8:16].unsqueeze(2).to_broadcast([128, 8, 16]), op=ALU.mult), R=[gate, zsum], W=[gate])
        em.op("dve", lambda e: e.tensor_scalar(out=eidf[:], in0=eidf[:], scalar1=0.0, scalar2=16383.0,
                                               op0=ALU.max, op1=ALU.min), R=[eidf], W=[eidf])
        if l > 0:
            em.op("dve", lambda e: e.tensor_scalar(out=eidf[:], in0=eidf[:], scalar1=float(l * 16384), scalar2=None,
                                                   op0=ALU.add), R=[eidf], W=[eidf])
        em.op("dve", lambda e: e.tensor_copy(out=eid[:], in_=eidf[:]), R=[eidf], W=[eid])
        em.op("dve", lambda e: e.memset(actv[:], 0.0), W=[actv])
        for j in range(128):
            g = gb[j % 4]
            em.dma("pool", g[:], peer_u, R=[eid], W=[g], fn=lambda e, g=g, j=j: e.indirect_dma_start(
                out=g[:], out_offset=None, in_=peer_u,
                in_offset=bass.IndirectOffsetOnAxis(ap=eid[:, j:j + 1], axis=0)))
            em.op("dve", lambda e, g=g, j=j: e.scalar_tensor_tensor(
                out=g[:], in0=g[:], scalar=1.0, in1=tmA[:], op0=ALU.mult, op1=ALU.mult,
                accum_out=actv[:, j:j + 1]), R=[g, tmA], W=[g, actv])
        em.op("act", lambda e: e.activation(out=ga[:], in_=actv[:], func=AF.Gelu), R=[actv], W=[ga])
        em.op("dve", lambda e: e.tensor_tensor(out=ga[:], in0=ga[:], in1=gate[:], op=ALU.mult), R=[ga, gate], W=[ga])
        for j in range(128):
            g = gb[j % 4]
            em.dma("pool", g[:], peer_v, R=[eid], W=[g], fn=lambda e, g=g, j=j: e.indirect_dma_start(
                out=g[:], out_offset=None, in_=peer_v,
                in_offset=bass.IndirectOffsetOnAxis(ap=eid[:, j:j + 1], axis=0)))
            acc = tmC if j % 2 == 0 else tmB
            if j < 2:
                em.op("dve", lambda e, g=g, j=j, acc=acc: e.tensor_scalar(out=acc[:], in0=g[:], scalar1=ga[:, j:j + 1],
                                                                          scalar2=None, op0=ALU.mult), R=[g, ga], W=[acc])
            else:
                em.op("dve", lambda e, g=g, j=j, acc=acc: e.scalar_tensor_tensor(
                    out=acc[:], in0=g[:], scalar=ga[:, j:j + 1], in1=acc[:], op0=ALU.mult, op1=ALU.add),
                    R=[g, ga, acc], W=[acc])
        em.op("dve", lambda e: e.tensor_tensor(out=tmC[:], in0=tmC[:], in1=tmB[:], op=ALU.add), R=[tmC, tmB], W=[tmC])
        em.op("dve", lambda e: e.tensor_tensor(out=tmC[:], in0=tmC[:], in1=grep[1][:], op=ALU.mult),
              R=[tmC, grep[1]], W=[tmC])
        em.op("dve", lambda e: e.tensor_tensor(out=x[:], in0=x[:], in1=tmC[:], op=ALU.add), R=[x, tmC], W=[x])

    def mix_out(wap):
        def cons_o(p, n0, nb):
            em.op("dve", lambda e: e.tensor_tensor(out=tmC[:, n0:n0 + nb], in0=p[:, 0:nb], in1=grep[0][:, n0:n0 + nb],
                                                   op=ALU.mult), R=[p, grep[0]], W=[tmC])
        linear(wap, D, cons_o)
        em.op("dve", lambda e: e.tensor_tensor(out=x[:], in0=x[:], in1=tmC[:], op=ALU.add), R=[x, tmC], W=[x])

    def gmlp(which, t):
        norm_mod_T(0, which)

        def cons_z(p, n0, nb):
            em.op("act", lambda e: e.activation(out=z[:, n0:n0 + nb], in_=p[:, 0:nb], func=AF.Gelu), R=[p], W=[z])
        linear(a_w_in, 2 * D, cons_z, bias_ap=a_b_in)
        rstd_of(z[:, D:2 * D], z, 1, D)
        em.op("dve", lambda e: e.scalar_tensor_tensor(out=tmA[:], in0=z[:, D:2 * D], scalar=small[:, 1:2], in1=vg_rep[:],
                                                      op0=ALU.mult, op1=ALU.mult), R=[z, small, vg_rep], W=[tmA])
        if t == NA - 1:
            em.dma("sp", gv_o[:, :], tmA[:], R=[tmA], W=[gv_o], ckey="out_gv")
        for g in range(16):
            pw = psW[g // 4]
            em.op("pe", lambda e, g=g, pw=pw: e.matmul(pw[:, (g % 4) * 128:(g % 4 + 1) * 128], lhsT=wsT[:, g, :],
                                                       rhs=tmA[:, g * 128:(g + 1) * 128], start=True, stop=True),
                  R=[wsT, tmA], W=[pw])
        for g in range(16):
            pw = psW[g // 4]
            em.op("dve", lambda e, g=g, pw=pw: e.scalar_tensor_tensor(
                out=tmB[:, g * 128:(g + 1) * 128], in0=pw[:, (g % 4) * 128:(g % 4 + 1) * 128], scalar=bsT[:, g:g + 1],
                in1=z[:, g * 128:(g + 1) * 128], op0=ALU.add, op1=ALU.mult), R=[pw, bsT, z], W=[tmB])
        transpose_to(hT, tmB)
        mix_out(a_w_out)

    def shared_kv(which, t):
        norm_mod_T(2, which)

        def cons_kv(p, n0, nb):
            em.op("act", lambda e: e.copy(out=z[:, n0:n0 + nb], in_=p[:, 0:nb]), R=[p], W=[z])
        linear(kv_w, 2 * D, cons_kv)
        rows = slice(t * 128, (t + 1) * 128)
        em.dma("sp", v_o[rows, :], z[:, D:2 * D], R=[z], W=[v_o], ckey="out_v")
        em.dma("sp", kv_all[rows, D:2 * D], z[:, D:2 * D], R=[z], W=["kv_all"], ckey="kv_all_w")
        head_norm(z[:, 0:D], z, kng, tmA)
        em.dma("sp", k_o[rows, :], tmA[:], R=[tmA], W=[k_o], ckey="out_k")
        em.dma("sp", kv_all[rows, 0:D], tmA[:], R=[tmA], W=["kv_all"], ckey="kv_all_w")

    def sb_block(hg, HG, nq, Kb, Vb, mask, first, last, s, o_fn):
        W = HG * nq
        base = "wsTs" if s == 0 else "vg_rep"
        bt = wsT[:].rearrange("p g t -> p (g t)") if s == 0 else vg_rep[:]
        ez, sp_, lm, a_ = (bt[:, i * 512:i * 512 + W] for i in range(4))
        kez, ksp, klm, ka = (base + "#" + n for n in ("ez", "sp", "lm", "a"))
        tt = att_t[:, s * 512:s * 512 + W]
        ktt = "att_t#%d" % s
        if HG == 4:
            ktk = "cand#kt%d" % s
            ktv = cand[:].rearrange("p h c -> p (h c)")[:, s * 512:(s + 1) * 512].rearrange("p (h k) -> p h k", k=128)
        else:
            ktk = "cand"
            ktv = cand[:].rearrange("p h c -> p (h c)").rearrange("p (h k) -> p h k", k=128)
        transpose_blocks(HG, lambda b: Kb[:, b * 128:(b + 1) * 128], [Kb],
                         lambda g0, n: ktv[:, g0:g0 + n, :], ktk)
        zp = psL[st["pl"]]
        lp = psL[st["pl"] ^ 1] if HG == 16 else psW[1 + s]
        if HG != 16:
            st["pl"] ^= 1
        for i in range(HG):
            em.op("pe", lambda e, i=i: e.matmul(zp[:, i * nq:(i + 1) * nq], lhsT=ktv[:, i, :],
                                                rhs=hT[:, hg * HG + i, 0:nq], start=True, stop=False),
                  R=[ktk, hT], W=[zp])
        brow = biasp[0:1, hg * 512:hg * 512 + W] if HG == 4 else biass[0:1, 0:W]
        em.op("pe", lambda e: e.matmul(zp[:, 0:W], lhsT=ones_row[0:1, :], rhs=brow, start=False, stop=True),
              R=[ones_row, biasp, biass], W=[zp])
        em.op("act", lambda e: e.activation(out=ez, in_=zp[:, 0:W], func=AF.Exp), R=[zp], W=[kez])
        em.op("act", lambda e: e.activation(out=sp_, in_=ez, func=AF.Ln, bias=1.0, scale=1.0), R=[kez], W=[ksp])
        v3 = lambda ap: ap.rearrange("p (h q) -> p h q", q=nq)
        if mask is None:
            em.op("dve", lambda e: e.tensor_scalar(out=lm, in0=sp_, scalar1=-1.0, scalar2=None, op0=ALU.mult),
                  R=[ksp], W=[klm])
        else:
            em.op("dve", lambda e: e.scalar_tensor_tensor(out=v3(lm), in0=v3(sp_), scalar=-1.0, in1=mask,
                                                          op0=ALU.mult, op1=ALU.mult), R=[ksp, amask, smask], W=[klm])
        em.op("pe", lambda e: e.matmul(lp[:, 0:W], lhsT=umat[:], rhs=lm, start=True, stop=first),
              R=[umat, klm], W=[lp])
        if not first:
            em.op("pe", lambda e: e.matmul(lp[:, 0:W], lhsT=ones128[:], rhs=lacc[:, 0:W], start=False, stop=True),
                  R=[ones128, lacc], W=[lp])
        em.op("dve", lambda e: e.tensor_tensor(out=tt, in0=zp[:, 0:W], in1=sp_, op=ALU.subtract), R=[zp, ksp], W=[ktt])
        em.op("dve", lambda e: e.tensor_tensor(out=tt, in0=lp[:, 0:W], in1=tt, op=ALU.add), R=[lp, ktt], W=[ktt])
        em.op("act", lambda e: e.activation(out=a_, in_=tt, func=AF.Exp), R=[ktt], W=[ka])
        if mask is not None:
            em.op("dve", lambda e: e.tensor_tensor(out=v3(a_), in0=v3(a_), in1=mask, op=ALU.mult),
                  R=[ka, amask, smask], W=[ka])
        if not last:
            if first:
                em.op("dve", lambda e: e.tensor_copy(out=lacc[:, 0:W], in_=lm), R=[klm], W=[lacc])
            else:
                em.op("dve", lambda e: e.tensor_tensor(out=lacc[:, 0:W], in0=lacc[:, 0:W], in1=lm, op=ALU.add),
                      R=[lacc, klm], W=[lacc])
        for i in range(HG):
            pt_, oap = o_fn(i)
            em.op("pe", lambda e, i=i, oap=oap: e.matmul(oap, lhsT=a_[:, i * nq:(i + 1) * nq],
                                                         rhs=Vb[:, i * 128:(i + 1) * 128], start=first, stop=last),
                  R=[ka, Vb], W=[pt_])

    def sbp_front(hg, Kb, mask, s):
        W = 512
        base = "wsTs" if s == 0 else "vg_rep"
        bt = wsT[:].rearrange("p g t -> p (g t)") if s == 0 else vg_rep[:]
        ez, sp_, lm, a_ = (bt[:, i * 512:i * 512 + W] for i in range(4))
        kez, ksp, klm, ka = (base + "#" + n for n in ("ez", "sp", "lm", "a"))
        ktk = "cand#kt%d" % s
        ktv = cand[:].rearrange("p h c -> p (h c)")[:, s * 512:(s + 1) * 512].rearrange("p (h k) -> p h k", k=128)
        transpose_blocks(4, lambda b: Kb[:, b * 128:(b + 1) * 128], [Kb], lambda g0, n: ktv[:, g0:g0 + n, :], ktk)
        zp = psL[s]
        for i in range(4):
            em.op("pe", lambda e, i=i: e.matmul(zp[:, i * 128:(i + 1) * 128], lhsT=ktv[:, i, :],
                                                rhs=hT[:, hg * 4 + i, :], start=True, stop=False), R=[ktk, hT], W=[zp])
        em.op("pe", lambda e: e.matmul(zp[:, 0:W], lhsT=ones_row[0:1, :], rhs=biasp[0:1, hg * 512:(hg + 1) * 512],
                                       start=False, stop=True), R=[ones_row, biasp], W=[zp])
        em.op("act", lambda e: e.activation(out=ez, in_=zp[:, 0:W], func=AF.Exp), R=[zp], W=[kez])
        em.op("act", lambda e: e.activation(out=sp_, in_=ez, func=AF.Ln, bias=1.0, scale=1.0), R=[kez], W=[ksp])
        v3 = lambda ap: ap.rearrange("p (h q) -> p h q", q=128)
        if mask is None:
            em.op("dve", lambda e: e.tensor_scalar(out=lm, in0=sp_, scalar1=-1.0, scalar2=None, op0=ALU.mult),
                  R=[ksp], W=[klm])
        else:
            em.op("dve", lambda e: e.scalar_tensor_tensor(out=v3(lm), in0=v3(sp_), scalar=-1.0, in1=mask,
                                                          op0=ALU.mult, op1=ALU.mult), R=[ksp, amask], W=[klm])
        return dict(s=s, zp=zp, sp_=sp_, lm=lm, a_=a_, ksp=ksp, klm=klm, ka=ka, mask=mask, v3=v3)

    def sbp_back(c, Vb, first, last):
        W = 512
        s, zp, sp_, lm, a_, ksp, klm, ka, mask, v3 = (c[k] for k in ("s", "zp", "sp_", "lm", "a_", "ksp", "klm", "ka", "mask", "v3"))
        lp = psW[1 + s]
        tt = att_t[:, s * 512:s * 512 + W]
        ktt = "att_t#%d" % s
        em.op("pe", lambda e: e.matmul(lp[:, 0:W], lhsT=umat[:], rhs=lm, start=True, stop=first), R=[umat, klm], W=[lp])
        if not first:
            em.op("pe", lambda e: e.matmul(lp[:, 0:W], lhsT=ones128[:], rhs=lacc[:, 0:W], start=False, stop=True),
                  R=[ones128, lacc], W=[lp])
        em.op("dve", lambda e: e.tensor_tensor(out=tt, in0=zp[:, 0:W], in1=sp_, op=ALU.subtract), R=[zp, ksp], W=[ktt])
        em.op("dve", lambda e: e.tensor_tensor(out=tt, in0=lp[:, 0:W], in1=tt, op=ALU.add), R=[lp, ktt], W=[ktt])
        em.op("act", lambda e: e.activation(out=a_, in_=tt, func=AF.Exp), R=[ktt], W=[ka])
        if mask is not None:
            em.op("dve", lambda e: e.tensor_tensor(out=v3(a_), in0=v3(a_), in1=mask, op=ALU.mult), R=[ka, amask], W=[ka])
        if not last:
            if first:
                em.op("dve", lambda e: e.tensor_copy(out=lacc[:, 0:W], in_=lm), R=[klm], W=[lacc])
            else:
                em.op("dve", lambda e: e.tensor_tensor(out=lacc[:, 0:W], in0=lacc[:, 0:W], in1=lm, op=ALU.add),
                      R=[lacc, klm], W=[lacc])
        for i in range(4):
            em.op("pe", lambda e, i=i: e.matmul(psW[0][:, i * 128:(i + 1) * 128], lhsT=a_[:, i * 128:(i + 1) * 128],
                                                rhs=Vb[:, i * 128:(i + 1) * 128], start=first, stop=last),
                  R=[ka, Vb], W=[psW[0]])

    def attn_prompt(j):
        nk = 4 * j + 4
        steps = [(hg, n, kc) for hg in range(4) for n, kc in enumerate(range(nk - 1, -1, -1))]

        def front(idx):
            hg, n, kc = steps[idx]
            s = idx % 2
            Kb, Vb = gb[s], gb[2 + s]
            rows = slice(kc * 128, (kc + 1) * 128)
            em.dma("sp", Kb[:, 0:512], kv_all[rows, hg * 512:(hg + 1) * 512], R=["kv_all"], W=[Kb])
            em.dma("sp", Vb[:, 0:512], kv_all[rows, D + hg * 512:D + (hg + 1) * 512], R=["kv_all"], W=[Vb])
            d = kc - 4 * j
            mask = None if d < 0 else amask[:, d * 128:(d + 1) * 128].unsqueeze(1).to_broadcast([128, 4, 128])
            return sbp_front(hg, Kb, mask, s), Vb

        nxt = front(0)
        for idx, (hg, n, kc) in enumerate(steps):
            cur = nxt
            if idx + 1 < len(steps):
                nxt = front(idx + 1)
            sbp_back(cur[0], cur[1], n == 0, n == nk - 1)
            if n == nk - 1:
                em.op("act", lambda e, hg=hg: e.copy(out=tmB[:, hg * 512:(hg + 1) * 512], in_=psW[0][:]),
                      R=[psW[0]], W=[tmB])

    def attn_sample():
        blocks = [None] + list(range(127, -1, -1))
        for n, pg in enumerate(blocks):
            s = n % 2
            Kb, Vb = gb[s], gb[2 + s]
            if pg is None:
                rows = slice((NA - 1) * 128, NA * 128)
                em.dma("sp", Kb[:], kv_all[rows, 0:D], R=["kv_all"], W=[Kb])
                em.dma("sp", Vb[:], kv_all[rows, D:2 * D], R=["kv_all"], W=[Vb])
                mask = smask[:].rearrange("p (h q) -> p h q", q=4)
            else:
                for buf, src in ((Kb, cache_k), (Vb, cache_v)):
                    em.dma("pool", buf[:], src, R=[pidx], W=[buf], fn=lambda e, buf=buf, src=src, pg=pg:
                           e.indirect_dma_start(out=buf[:], out_offset=None, in_=src,
                                                in_offset=bass.IndirectOffsetOnAxis(ap=pidx[:, pg:pg + 1], axis=0)))
                mask = None
            sb_block(0, 16, 4, Kb, Vb, mask, n == 0, n == len(blocks) - 1, s,
                     lambda i: (psW[i // 4], psW[i // 4][0:4, (i % 4) * 128:(i % 4 + 1) * 128]))
        em.op("dve", lambda e: e.memset(tmB[:], 0.0), W=[tmB])
        for q4 in range(4):
            em.op("act", lambda e, q4=q4: e.copy(out=tmB[0:4, q4 * 512:(q4 + 1) * 512], in_=psW[q4][0:4, :]),
                  R=[psW[q4]], W=[tmB])

    def layer1(which, attn):
        norm_mod_T(3, which)

        def cons_q(p, n0, nb):
            em.op("act", lambda e: e.copy(out=z[:, n0:n0 + nb], in_=p[:, 0:nb]), R=[p], W=[z])
        linear(b_w_q, D, cons_q)
        head_norm(z[:, 0:D], z, qng, tmA, extra=128.0 ** -0.5)
        transpose_to(hT, tmA)
        attn()
        transpose_to(hT, tmB)
        mix_out(b_w_o)
        peer(1, which, 4)

    for which, tiles in ((0, range(0, NP)), (1, range(NP, NA))):
        gate_rep(0, which, 2, grep[0])
        gate_rep(0, which, 5, grep[1])
        for t in tiles:
            src = xall[t * 128:(t + 1) * 128, :] if t < NP else xsmp
            em.dma("sp", x[:], src, R=[], W=[x])
            gmlp(which, t)
            peer(0, which, 1)
            shared_kv(which, t)
            em.dma("sp", xs_all[t * 128:(t + 1) * 128, :], x[:], R=[x], W=["xs_all"], ckey="xs_all_w")
    for which, tiles in ((0, range(0, 8)), (1, range(8, 9))):
        gate_rep(1, which, 2, grep[0])
        gate_rep(1, which, 5, grep[1])
        for t in tiles:
            if t < 8:
                em.dma("pool", x[:], xs_all, R=["xs_all", myrows], W=[x], fn=lambda e, t=t: e.indirect_dma_start(
                    out=x[:], out_offset=None, in_=xs_all,
                    in_offset=bass.IndirectOffsetOnAxis(ap=myrows[:, t:t + 1], axis=0)))
                layer1(which, lambda t=t: attn_prompt(t))
            else:
                em.dma("sp", x[:], xs_all[(NA - 1) * 128:NA * 128, :], R=["xs_all"], W=[x])
                layer1(which, attn_sample)
            em.dma("sp", y_o[t * 128:(t + 1) * 128, :], x[:], R=[x], W=[y_o], ckey="out_y")

    em.wait_all_dma("sp")
    em.replay()
    return nc, es


def _fm(vec):
    return np.ascontiguousarray(vec.reshape(-1, 128).T)


def kernel(x_prompt, x_sample, cache_k, cache_v, page_table, c_prompt, c_sample,
           mod_w, mod_b, norm_mix_g, norm_ffn_g,
           a_w_in, a_b_in, a_v_norm_g, a_w_s, a_b_s, a_w_out,
           kv_mod_w, kv_mod_b, kv_norm_g, kv_w, k_norm_g,
           b_w_q, b_q_norm_g, b_logit_bias, b_w_o,
           peer_w_q, peer_subkeys, peer_u, peer_v):
    f = lambda a: np.ascontiguousarray(np.asarray(a, dtype=np.float32))
    x_prompt, x_sample, c_prompt, c_sample = f(x_prompt), f(x_sample), f(c_prompt), f(c_sample)
    mod_w, mod_b = f(mod_w), f(mod_b)
    lb = f(b_logit_bias)[0]
    kq = np.arange(128)
    shared = {
        "mod_w": mod_w, "mod_b": mod_b,
        "mod_bT": np.stack([_fm(mod_b[l]) for l in range(2)]),
        "normT": np.concatenate([_fm(f(norm_mix_g)[0]), _fm(f(norm_ffn_g)[0]), _fm(f(kv_norm_g)),
                                 _fm(f(norm_mix_g)[1]), _fm(f(norm_ffn_g)[1])], axis=1),
        "a_w_in": f(a_w_in)[0], "a_b_in": f(a_b_in)[0:1], "a_vg": f(a_v_norm_g)[0:1],
        "wsT": np.ascontiguousarray(np.transpose(f(a_w_s)[0], (2, 0, 1)).reshape(128, 16 * 128)),
        "bsT": np.ascontiguousarray(f(a_b_s)[0].T),
        "a_w_out": f(a_w_out)[0],
        "kv_mod_w": f(kv_mod_w), "kv_mod_bT": _fm(f(kv_mod_b)), "kv_w": f(kv_w),
        "k_norm_g": f(k_norm_g).reshape(1, 128), "q_norm_g": f(b_q_norm_g)[0].reshape(1, 128),
        "b_w_q": f(b_w_q)[0], "b_w_o": f(b_w_o)[0],
        "peer_w_q": f(peer_w_q),
        "skT": np.ascontiguousarray(np.transpose(f(peer_subkeys), (0, 3, 1, 2)).reshape(2, 128, 256)),
        "peer_u": f(peer_u).reshape(2 * 16384, D), "peer_v": f(peer_v).reshape(2 * 16384, D),
        "cache_k": f(cache_k).reshape(1280 * 128, D), "cache_v": f(cache_v).reshape(1280 * 128, D),
        "ident": np.eye(128, dtype=np.float32),
        "tril": np.triu(np.ones((128, 128), np.float32)),
        "umat": np.tril(np.ones((128, 128), np.float32), -1),
        "ones128": np.ones((128, 128), np.float32),
        "smask": np.ascontiguousarray(np.broadcast_to((kq[:, None] < np.arange(4)[None, :]).astype(np.float32)[:, None, :],
                                                      (128, 16, 4)).reshape(128, 64)),
        "biasrow_p": np.ascontiguousarray(np.repeat(lb, 128).reshape(1, D)),
        "biasrow_s": np.ascontiguousarray(np.repeat(lb, 4).reshape(1, 64)),
        "iota_p": np.arange(128, dtype=np.float32).reshape(128, 1),
    }
    strict = (kq[:, None] < kq[None, :]).astype(np.float32)
    in_maps = []
    for c in range(8):
        b, r = c // 4, c % 4
        xs = np.zeros((128, D), np.float32)
        xs[0:4] = x_sample[c]
        am = np.zeros((128, 4, 128), np.float32)
        for d in range(4):
            am[:, d, :] = 1.0 if d < r else (strict if d == r else 0.0)
        m = dict(shared)
        m["xall"] = x_prompt[b]
        m["xsmp"] = xs
        m["c2T"] = np.ascontiguousarray(np.stack([_fm(c_prompt[b]), _fm(c_sample[c])], axis=2).reshape(128, KT * 2))
        m["pt"] = np.ascontiguousarray(np.asarray(page_table, dtype=np.int32)[c].reshape(1, 128))
        m["myrows"] = np.ascontiguousarray(((4 * np.arange(8)[None, :] + r) * 128 + kq[:, None]).astype(np.int32))
        m["amask"] = np.ascontiguousarray(am.reshape(128, 512))
        in_maps.append(m)

    nc, es = build_program()
    with es:
        res = run_bass_kernel_spmd(nc, in_maps, core_ids=list(range(8)))
    R = res.results

    y_prompt = np.zeros((2, 4096, D), np.float32)
    k_prompt = np.zeros((2, 4096, 16, 128), np.float32)
    v_prompt = np.zeros((2, 4096, 16, 128), np.float32)
    y_sample = np.zeros((8, 4, D), np.float32)
    k_sample = np.zeros((8, 4, 16, 128), np.float32)
    v_sample = np.zeros((8, 4, 16, 128), np.float32)
    gmlp_v = np.zeros((1, 8, 4, D), np.float32)
    for c in range(8):
        b, r = c // 4, c % 4
        y, ko, vo, gv = (np.asarray(R[c][k]) for k in ("y", "ko", "vo", "gv"))
        for j in range(8):
            ch = 4 * j + r
            y_prompt[b, ch * 128:(ch + 1) * 128] = y[j * 128:(j + 1) * 128]
            k_prompt[b, ch * 128:(ch + 1) * 128] = ko[ch * 128:(ch + 1) * 128].reshape(128, 16, 128)
            v_prompt[b, ch * 128:(ch + 1) * 128] = vo[ch * 128:(ch + 1) * 128].reshape(128, 16, 128)
        y_sample[c] = y[1024:1028]
        k_sample[c] = ko[NP * 128:NP * 128 + 4].reshape(4, 16, 128)
        v_sample[c] = vo[NP * 128:NP * 128 + 4].reshape(4, 16, 128)
        gmlp_v[0, c] = gv[0:4]
    return (y_prompt, y_sample, k_prompt, v_prompt, k_sample, v_sample, gmlp_v)
```

```python
from contextlib import ExitStack
import numpy as np
import concourse.bass as bass
import concourse.mybir as mybir
from concourse.bass_utils import run_bass_kernel_spmd

F32 = mybir.dt.float32
I32 = mybir.dt.int32
U32 = mybir.dt.uint32
ALU = mybir.AluOpType
AF = mybir.ActivationFunctionType
AX = mybir.AxisListType

D = 2048
KT = 16
NT = 9
NP = 32
NA = NP + 1
NB = 256
EPS = 1e-6
NEG = -1.0e30
SAME_ENGINE_RAW_SYNC = True


def _key(x):
    if isinstance(x, str):
        return x
    n = getattr(x, "name", None)
    if isinstance(n, str):
        return n
    return x.tensor.name


class Em:
    def __init__(self, nc, es):
        self.nc, self.es = nc, es
        self.engs = ["pe", "act", "dve", "pool", "sp"]
        self.ops = {e: [] for e in self.engs}
        self.psem = {e: es.enter_context(nc.semaphore("prog_" + e)) for e in self.engs}
        self.cnt = {e: 0 for e in self.engs}
        self.waited = {}
        self.bufs = {}
        self.chan = {}
        self.untracked = set()

    def sb(self, name, shape, dt=F32):
        return self.es.enter_context(self.nc.sbuf_tensor(name, list(shape), dt))

    def ps(self, name, shape, dt=F32):
        return self.es.enter_context(self.nc.psum_tensor(name, list(shape), dt))

    def _st(self, k):
        return self.bufs.setdefault(k, {"w": None, "r": []})

    def _tokval(self, tok):
        if tok[0] == "eng":
            return ("p", tok[1]), self.psem[tok[1]], tok[2]
        ch = self.chan[tok[1]]
        return ("c", tok[1]), ch[0], ch[1]

    def _rel(self, k):
        if "#" in k:
            return [k, k.split("#")[0]], [k]
        subs = [q for q in self.bufs if q.startswith(k + "#")]
        return [k] + subs, [k] + subs

    def _deps(self, eng, R, W):
        R = [q for k in R if k not in self.untracked for q in self._rel(k)[0]]
        W = [q for k in W for q in self._rel(k)[0]]
        waits = []

        def need(tok, raw):
            if tok is None:
                return
            if tok[0] == "eng" and tok[1] == eng:
                if not (raw and SAME_ENGINE_RAW_SYNC and eng in ("act", "dve", "pool")):
                    return
            sk, sem, val = self._tokval(tok)
            if self.waited.get((eng, sk), 0) >= val:
                return
            self.waited[(eng, sk)] = val
            waits.append((sem, val))

        for k in R:
            if k in self.untracked:
                continue
            need(self._st(k)["w"], True)
        for k in W:
            st = self._st(k)
            need(st["w"], False)
            for t in st["r"]:
                need(t, False)
        return waits

    def _commit(self, tok, R, W):
        R = [q for k in R if k not in self.untracked for q in self._rel(k)[1]]
        W = [q for k in W for q in self._rel(k)[1]]
        for k in R:
            if k in self.untracked:
                continue
            self._st(k)["r"].append(tok)
        for k in W:
            st = self._st(k)
            st["w"] = tok
            st["r"] = []

    def op(self, eng, fn, R=(), W=()):
        R = [_key(x) for x in R]
        W = [_key(x) for x in W]
        waits = self._deps(eng, R, W)
        self.cnt[eng] += 1
        tok = ("eng", eng, self.cnt[eng])
        self.ops[eng].append((waits, fn, (self.psem[eng], 1)))
        self._commit(tok, R, W)

    def dma(self, q, out, in_, R=None, W=None, fn=None, ckey=None):
        Rk = [_key(x) for x in (R if R is not None else [in_])]
        Wk = [_key(x) for x in (W if W is not None else [out])]
        kind = "sw" if q == "pool" else "hw"
        ck = (ckey or (Wk[0] if Wk[0] not in self.untracked_out else Rk[0]), kind)
        if ck not in self.chan:
            self.chan[ck] = [self.es.enter_context(self.nc.semaphore("ch%d" % len(self.chan))), 0]
        waits = self._deps(q, Rk, Wk)
        ch = self.chan[ck]
        ch[1] += 16
        if fn is None:
            fn = lambda e, o=out, i=in_: e.dma_start(out=o, in_=i)
        self.ops[q].append((waits, fn, (ch[0], 16)))
        self._commit(("dma", ck), Rk, Wk)

    untracked_out = set()

    def wait_all_dma(self, eng="sp"):
        waits = []
        for ck, ch in self.chan.items():
            if ch[1] > 0:
                waits.append((ch[0], ch[1]))
        self.ops[eng].append((waits, None, None))

    def replay(self):
        with self.nc.Block() as blk:
            def run(e, name):
                for waits, fn, inc in self.ops[name]:
                    for sem, val in waits:
                        e.wait_ge(sem, val)
                    if fn is None:
                        continue
                    ins = fn(e)
                    if inc is not None:
                        ins.then_inc(inc[0], inc[1])

            @blk.tensor
            def _(e):
                run(e, "pe")

            @blk.scalar
            def _(e):
                run(e, "act")

            @blk.vector
            def _(e):
                run(e, "dve")

            @blk.gpsimd
            def _(e):
                run(e, "pool")

            @blk.sync
            def _(e):
                run(e, "sp")


def build_program():
    nc = bass.Bass("TRN2", target_bir_lowering=False)
    es = ExitStack()
    em = Em(nc, es)

    def din(name, shape, dt=F32):
        t = nc.dram_tensor(name, list(shape), dt, kind="ExternalInput")
        em.untracked.add(name)
        return t.ap()

    def dout(name, shape, dt=F32):
        return nc.dram_tensor(name, list(shape), dt, kind="ExternalOutput").ap()

    xall = din("xall", [NP * 128, D])
    xsmp = din("xsmp", [128, D])
    c2T = din("c2T", [128, KT * 2])
    pt_d = din("pt", [1, 128], I32)
    myrows_d = din("myrows", [128, 8], I32)
    amask_d = din("amask", [128, 4 * 128])
    smask_d = din("smask", [128, 64])
    umat_d = din("umat", [128, 128])
    ones_d = din("ones128", [128, 128])
    biasp_d = din("biasrow_p", [1, D])
    biass_d = din("biasrow_s", [1, 64])
    iota_d = din("iota_p", [128, 1])
    mod_w = din("mod_w", [2, D, 6 * D])
    mod_bT = din("mod_bT", [2, 128, 96])
    mod_b = din("mod_b", [2, 6 * D])
    normT = din("normT", [128, 5 * KT])
    a_w_in = din("a_w_in", [D, 2 * D])
    a_b_in = din("a_b_in", [1, 2 * D])
    a_vg = din("a_vg", [1, D])
    wsT_d = din("wsT", [128, 16 * 128])
    bsT_d = din("bsT", [128, 16])
    a_w_out = din("a_w_out", [D, D])
    kv_mod_w = din("kv_mod_w", [D, 2 * D])
    kv_mod_bT = din("kv_mod_bT", [128, 32])
    kv_w = din("kv_w", [D, 2 * D])
    kng_d = din("k_norm_g", [1, 128])
    qng_d = din("q_norm_g", [1, 128])
    b_w_q = din("b_w_q", [D, D])
    b_w_o = din("b_w_o", [D, D])
    peer_w_q = din("peer_w_q", [2, D, D])
    skT_d = din("skT", [2, 128, 2 * 128])
    peer_u = din("peer_u", [2 * 16384, D])
    peer_v = din("peer_v", [2 * 16384, D])
    cache_k = din("cache_k", [1280 * 128, D])
    cache_v = din("cache_v", [1280 * 128, D])
    ident_d = din("ident", [128, 128])
    tril_d = din("tril", [128, 128])

    y_o = dout("y", [NT * 128, D])
    k_o = dout("ko", [NA * 128, D])
    v_o = dout("vo", [NA * 128, D])
    gv_o = dout("gv", [128, D])
    kv_all = nc.dram_tensor("kv_all", [NA * 128, 2 * D], F32).ap()
    xs_all = nc.dram_tensor("xs_all", [NA * 128, D], F32).ap()

    x = em.sb("x", [128, D])
    tmA = em.sb("tmA", [128, D])
    tmB = em.sb("tmB", [128, D])
    tmC = em.sb("tmC", [128, D])
    hT = em.sb("hT", [128, KT, 128])
    z = em.sb("z", [128, 2 * D])
    wbuf = [em.sb("wbuf%d" % i, [128, KT + 1, NB]) for i in range(2)]
    wsT = em.sb("wsTs", [128, 16, 128])
    vg_rep = em.sb("vg_rep", [128, D])
    grep = [em.sb("grep%d" % i, [128, D]) for i in range(2)]
    cand = em.sb("cand", [128, 8, 256])
    cidx = em.sb("cidx", [128, 8, 256])
    gb = [em.sb("gb%d" % i, [128, D]) for i in range(4)]
    ident = em.sb("ident_s", [128, 128])
    tril = em.sb("tril_s", [128, 128])
    ones_row = em.sb("ones_row", [1, 128])
    scT = em.sb("scT", [128, KT * 2])
    modT = [em.sb("modT%d" % l, [128, 96 * 2]) for l in range(2)]
    kvmodT = em.sb("kvmodT", [128, 32 * 2])
    mbT = [em.sb("mbT%d" % l, [128, 96]) for l in range(2)]
    kvmbT = em.sb("kvmbT", [128, 32])
    nrmT = em.sb("nrmT", [128, 5 * KT])
    coefA = em.sb("coefA", [128, 5 * 2 * KT])
    coefB = em.sb("coefB", [128, 5 * 2 * KT])
    bsT = em.sb("bsTs", [128, 16])
    kng = em.sb("kng", [128, 128])
    qng = em.sb("qng", [128, 128])
    skT = [em.sb("skT%d" % l, [128, 256]) for l in range(2)]
    small = em.sb("small", [128, 64])
    top = em.sb("top", [128, 16, 16])
    tidx = em.sb("tidx", [128, 16, 16], U32)
    tidxf = em.sb("tidxf", [128, 16, 16])
    mr = em.sb("mr", [128, 256])
    tops = em.sb("tops", [128, 8, 16])
    eidf = em.sb("eidf", [128, 128])
    eid = em.sb("eid", [128, 128], I32)
    gate = em.sb("gate", [128, 128])
    actv = em.sb("actv", [128, 128])
    ga = em.sb("ga", [128, 128])
    zsum = em.sb("zsum", [128, 16])
    att_t = em.sb("att_t", [128, 2 * 512])
    lacc = em.sb("lacc", [128, 512])
    amask = em.sb("amask_s", [128, 4 * 128])
    smask = em.sb("smask_s", [128, 64])
    umat = em.sb("umat_s", [128, 128])
    ones128 = em.sb("ones128_s", [128, 128])
    biasp = em.sb("biasp", [1, D])
    biass = em.sb("biass", [1, 64])
    iota_p = em.sb("iota_ps", [128, 1])
    pti = em.sb("pti", [128, 128], I32)
    ptf = em.sb("ptf", [128, 128])
    pidx = em.sb("pidx", [128, 128], I32)
    myrows = em.sb("myrows_s", [128, 8], I32)

    psL = [em.ps("psL%d" % i, [128, 512]) for i in range(2)]
    psT = [em.ps("psT%d" % i, [128, 512]) for i in range(2)]
    psW = [em.ps("psW%d" % i, [128, 512]) for i in range(4)]

    st = {"wslot": 0, "pl": 0, "pt": 0}

    def load_w(wap, n0, nb, bias_ap=None):
        s = st["wslot"]
        st["wslot"] ^= 1
        wb = wbuf[s]
        em.dma("sp", wb[:, 0:KT, 0:nb], wap[:, n0:n0 + nb].rearrange("(kt p) n -> p kt n", p=128),
               R=[], W=[wb])
        if bias_ap is not None:
            em.dma("sp", wb[0:1, KT, 0:nb], bias_ap[0:1, n0:n0 + nb], R=[], W=[wb])
        return wb

    def linear(wap, ncols, consume, bias_ap=None, lhs=None):
        lhs = lhs if lhs is not None else hT
        nblk = ncols // NB
        nxt = load_w(wap, 0, NB, bias_ap)
        for b in range(nblk):
            wb = nxt
            if b + 1 < nblk:
                nxt = load_w(wap, (b + 1) * NB, NB, bias_ap)
            p = psL[st["pl"]]
            st["pl"] ^= 1
            for kt in range(KT):
                last = (kt == KT - 1) and bias_ap is None
                em.op("pe", lambda e, p=p, wb=wb, kt=kt, last=last: e.matmul(
                    p[:, 0:NB], lhsT=lhs[:, kt, :], rhs=wb[:, kt, 0:NB], start=(kt == 0), stop=last),
                    R=[lhs, wb], W=[p])
            if bias_ap is not None:
                em.op("pe", lambda e, p=p, wb=wb: e.matmul(
                    p[:, 0:NB], lhsT=ones_row[0:1, :], rhs=wb[0:1, KT, 0:NB], start=False, stop=True),
                    R=[ones_row, wb], W=[p])
            consume(p, b * NB, NB)

    def transpose_blocks(nblk, src_fn, srckeys, dst_fn, dstkey, coef=None):
        for g0 in range(0, nblk, 4):
            n = min(4, nblk - g0)
            p = psT[st["pt"]]
            st["pt"] ^= 1
            for i in range(n):
                em.op("pe", lambda e, p=p, i=i, blk=g0 + i: e.transpose(
                    p[:, i * 128:(i + 1) * 128], src_fn(blk), ident[:]), R=list(srckeys) + [ident], W=[p])
            if coef is None:
                em.op("act", lambda e, p=p, g0=g0, n=n: e.copy(
                    out=dst_fn(g0, n), in_=p[:, 0:n * 128].rearrange("p (a b) -> p a b", a=n)),
                    R=[p], W=[dstkey])
            else:
                A, B = coef
                for i in range(n):
                    kt = g0 + i
                    em.op("act", lambda e, p=p, i=i, kt=kt, A=A, B=B: e.activation(
                        out=dst_fn(kt, 1)[:, 0, :], in_=p[:, i * 128:(i + 1) * 128], func=AF.Identity,
                        bias=B[:, kt:kt + 1], scale=A[:, kt:kt + 1]),
                        R=[p, coefA, coefB], W=[dstkey])

    def transpose_to(dst, src, coef=None):
        transpose_blocks(KT, lambda b: src[:, b * 128:(b + 1) * 128], [src],
                         lambda g0, n: dst[:, g0:g0 + n, :], dst, coef)

    def transpose_back(dst, srcT):
        transpose_blocks(KT, lambda b: srcT[:, b, :], [srcT],
                         lambda g0, n: dst[:, g0 * 128:(g0 + n) * 128].rearrange("p (a b) -> p a b", a=n), dst)

    def rsqrt_cols(c0, c1, n):
        em.op("dve", lambda e: e.tensor_scalar(out=small[:, c0:c1], in0=small[:, c0:c1],
                                               scalar1=1.0 / n, scalar2=EPS, op0=ALU.mult, op1=ALU.add),
              R=[small], W=[small])
        em.op("act", lambda e: e.activation(out=small[:, c0:c1], in_=small[:, c0:c1], func=AF.Sqrt),
              R=[small], W=[small])
        em.op("dve", lambda e: e.reciprocal(out=small[:, c0:c1], in_=small[:, c0:c1]), R=[small], W=[small])

    def rstd_of(src, srckey, col, n):
        em.op("dve", lambda e: e.memset(small[:, col:col + 1], 0.0), W=[small])
        em.op("act", lambda e: e.activation(out=tmC[:, 0:n], in_=src, func=AF.Square,
                                             accum_out=small[:, col:col + 1]), R=[srckey], W=[tmC, small])
        rsqrt_cols(col, col + 1, n)

    def norm_mod_T(kind, which):
        rstd_of(x[:], x, 0, D)
        em.op("dve", lambda e: e.tensor_scalar(out=tmA[:], in0=x[:], scalar1=small[:, 0:1], scalar2=None,
                                               op0=ALU.mult), R=[x, small], W=[tmA])
        o = (kind * 2 + which) * KT
        transpose_to(hT, tmA, coef=(coefA[:, o:o + KT], coefB[:, o:o + KT]))

    def head_norm(src, srckey, gain, dst, extra=1.0):
        d3 = dst[:].rearrange("p (h d) -> p h d", d=128)
        em.op("dve", lambda e: e.tensor_tensor(out=dst[:], in0=src, in1=src, op=ALU.mult), R=[srckey], W=[dst])
        em.op("dve", lambda e: e.tensor_reduce(out=small[:, 16:32], in_=d3, axis=AX.X, op=ALU.add), R=[dst], W=[small])
        rsqrt_cols(16, 32, 128)
        if extra != 1.0:
            em.op("dve", lambda e: e.tensor_scalar(out=small[:, 16:32], in0=small[:, 16:32], scalar1=extra, scalar2=None,
                                                   op0=ALU.mult), R=[small], W=[small])
        em.op("dve", lambda e: e.tensor_tensor(out=d3, in0=src.rearrange("p (h d) -> p h d", d=128),
                                               in1=small[:, 16:32].unsqueeze(2).to_broadcast([128, 16, 128]),
                                               op=ALU.mult), R=[srckey, small], W=[dst])
        em.op("dve", lambda e: e.tensor_tensor(out=d3, in0=d3, in1=gain[:].unsqueeze(1).to_broadcast([128, 16, 128]),
                                               op=ALU.mult), R=[dst, gain], W=[dst])

    for dst, src in ((ident, ident_d), (tril, tril_d), (scT, c2T), (nrmT, normT), (bsT, bsT_d),
                     (kvmbT, kv_mod_bT), (mbT[0], mod_bT[0]), (mbT[1], mod_bT[1]),
                     (skT[0], skT_d[0]), (skT[1], skT_d[1]), (amask, amask_d), (smask, smask_d),
                     (umat, umat_d), (ones128, ones_d), (biasp, biasp_d), (biass, biass_d),
                     (iota_p, iota_d), (myrows, myrows_d)):
        em.dma("sp", dst[:], src, R=[], W=[dst])
    em.dma("sp", wsT[:].rearrange("p g t -> p (g t)"), wsT_d, R=[], W=[wsT])
    em.dma("sp", vg_rep[:], a_vg.partition_broadcast(128), R=[], W=[vg_rep])
    em.dma("sp", kng[:], kng_d.partition_broadcast(128), R=[], W=[kng])
    em.dma("sp", qng[:], qng_d.partition_broadcast(128), R=[], W=[qng])
    em.dma("sp", pti[:], pt_d.partition_broadcast(128), R=[], W=[pti])
    em.op("dve", lambda e: e.memset(ones_row[:], 1.0), W=[ones_row])
    em.op("dve", lambda e: e.tensor_copy(out=ptf[:], in_=pti[:]), R=[pti], W=[ptf])
    em.op("dve", lambda e: e.tensor_scalar(out=pidx[:], in0=ptf[:], scalar1=128.0, scalar2=iota_p[:, 0:1],
                                           op0=ALU.mult, op1=ALU.add), R=[ptf, iota_p], W=[pidx])
    em.op("dve", lambda e: e.tensor_tensor(out=wsT[:], in0=wsT[:],
                                           in1=tril[:].unsqueeze(1).to_broadcast([128, 16, 128]), op=ALU.mult),
          R=[wsT, tril], W=[wsT])
    em.op("act", lambda e: e.activation(out=scT[:], in_=scT[:], func=AF.Silu), R=[scT], W=[scT])

    def mod_T(wap, ntiles, outT, biasT):
        pm = psW[0]
        nblk = ntiles * 128 // NB
        nxt = load_w(wap, 0, NB)
        for b in range(nblk):
            wb = nxt
            if b + 1 < nblk:
                nxt = load_w(wap, (b + 1) * NB, NB)
            for i in range(NB // 128):
                nt = b * (NB // 128) + i
                for kt in range(KT):
                    em.op("pe", lambda e, wb=wb, i=i, nt=nt, kt=kt: e.matmul(
                        pm[:, nt * 2:nt * 2 + 2], lhsT=wb[:, kt, i * 128:(i + 1) * 128],
                        rhs=scT[:, kt * 2:kt * 2 + 2], start=(kt == 0), stop=(kt == KT - 1)),
                        R=[wb, scT], W=[pm])
        em.op("dve", lambda e: e.tensor_tensor(
            out=outT[:].rearrange("p (n w) -> p n w", w=2), in0=pm[:, 0:ntiles * 2].rearrange("p (n w) -> p n w", w=2),
            in1=biasT[:].unsqueeze(2).to_broadcast([128, ntiles, 2]), op=ALU.add), R=[pm, biasT], W=[outT])

    mod_T(mod_w[0], 96, modT[0], mbT[0])
    mod_T(kv_mod_w, 32, kvmodT, kvmbT)
    mod_T(mod_w[1], 96, modT[1], mbT[1])

    def coefs(kind, mT, sh0, sc0):
        for w in range(2):
            o = (kind * 2 + w) * KT
            mv = mT[:].rearrange("p (n w) -> p n w", w=2)
            em.op("dve", lambda e, o=o, w=w, mv=mv: e.scalar_tensor_tensor(
                out=coefA[:, o:o + KT], in0=mv[:, sc0:sc0 + KT, w], scalar=1.0,
                in1=nrmT[:, kind * KT:(kind + 1) * KT], op0=ALU.add, op1=ALU.mult),
                R=[mT, nrmT], W=[coefA])
            em.op("dve", lambda e, o=o, w=w, mv=mv: e.tensor_copy(
                out=coefB[:, o:o + KT], in_=mv[:, sh0:sh0 + KT, w]), R=[mT], W=[coefB])

    coefs(0, modT[0], 0, 16)
    coefs(1, modT[0], 48, 64)
    coefs(2, kvmodT, 0, 16)
    coefs(3, modT[1], 0, 16)
    coefs(4, modT[1], 48, 64)

    class _Rep:
        name = "tmB"

        def __getitem__(self, idx):
            return tmB[:].rearrange("p (k m) -> p k m", m=128)[idx]
    rep_h = _Rep()

    def gate_rep(l, which, gi, dst):
        rep = tmB[:].rearrange("p (k m) -> p k m", m=128)
        em.op("dve", lambda e: e.tensor_copy(
            out=rep, in_=scT[:].rearrange("p (k w) -> p k w", w=2)[:, :, which:which + 1].to_broadcast([128, KT, 128])),
            R=[scT], W=[tmB])
        em.dma("sp", tmC[:], mod_b[l:l + 1, gi * D:(gi + 1) * D].partition_broadcast(128), R=[], W=[tmC])

        def cons(p, n0, nb):
            em.op("dve", lambda e: e.tensor_tensor(out=dst[:, n0:n0 + nb], in0=p[:, 0:nb], in1=tmC[:, n0:n0 + nb],
                                                   op=ALU.add), R=[p, tmC], W=[dst])
        linear(mod_w[l][:, gi * D:(gi + 1) * D], D, cons, lhs=rep_h)

    def peer(l, which, kind):
        norm_mod_T(kind, which)

        def cons_q(p, n0, nb):
            em.op("act", lambda e: e.copy(out=z[:, n0:n0 + nb], in_=p[:, 0:nb]), R=[p], W=[z])
        linear(peer_w_q[l], D, cons_q)
        transpose_back(tmA, hT)
        transpose_to(hT, z)
        for hp in range(16):
            pcol = hp % 2
            pw = psW[hp // 4]
            em.op("pe", lambda e, hp=hp, pcol=pcol, pw=pw: e.matmul(
                pw[:, (hp % 4) * 128:(hp % 4 + 1) * 128], lhsT=hT[:, hp, :], rhs=skT[l][:, pcol * 128:(pcol + 1) * 128],
                start=True, stop=True), R=[hT, skT[l]], W=[pw])
        for q4 in range(4):
            em.op("act", lambda e, q4=q4: e.copy(out=tmB[:, q4 * 512:(q4 + 1) * 512], in_=psW[q4][:]),
                  R=[psW[q4]], W=[tmB])
        for hp in range(16):
            sv = tmB[:, hp * 128:(hp + 1) * 128]
            em.op("dve", lambda e, hp=hp, sv=sv: e.max(out=top[:, hp, 0:8], in_=sv), R=[tmB], W=[top])
            em.op("dve", lambda e, hp=hp, sv=sv: e.max_index(out=tidx[:, hp, 0:8], in_max=top[:, hp, 0:8], in_values=sv),
                  R=[tmB, top], W=[tidx])
            em.op("dve", lambda e, hp=hp, sv=sv: e.match_replace(out=mr[:, 0:128], in_to_replace=top[:, hp, 0:8],
                                                                in_values=sv, imm_value=NEG), R=[tmB, top], W=[mr])
            em.op("dve", lambda e, hp=hp: e.max(out=top[:, hp, 8:16], in_=mr[:, 0:128]), R=[mr], W=[top])
            em.op("dve", lambda e, hp=hp: e.max_index(out=tidx[:, hp, 8:16], in_max=top[:, hp, 8:16],
                                                      in_values=mr[:, 0:128]), R=[mr, top], W=[tidx])
        em.op("dve", lambda e: e.tensor_copy(out=tidxf[:], in_=tidx[:]), R=[tidx], W=[tidxf])
        oh = z[:].rearrange("p (k c) -> p k c", c=256)
        for h in range(8):
            c3 = cand[:, h, :].rearrange("p (a b) -> p a b", b=16)
            i3 = cidx[:, h, :].rearrange("p (a b) -> p a b", b=16)
            em.op("dve", lambda e, h=h, c3=c3: e.tensor_tensor(
                out=c3, in0=top[:, 2 * h, :].unsqueeze(2).to_broadcast([128, 16, 16]),
                in1=top[:, 2 * h + 1, :].unsqueeze(1).to_broadcast([128, 16, 16]), op=ALU.add), R=[top], W=[cand])
            em.op("dve", lambda e, h=h, i3=i3: e.scalar_tensor_tensor(
                out=i3, in0=tidxf[:, 2 * h, :].unsqueeze(2).to_broadcast([128, 16, 16]), scalar=128.0,
                in1=tidxf[:, 2 * h + 1, :].unsqueeze(1).to_broadcast([128, 16, 16]), op0=ALU.mult, op1=ALU.add),
                R=[tidxf], W=[cidx])
            em.op("dve", lambda e, h=h: e.max(out=tops[:, h, 0:8], in_=cand[:, h, :]), R=[cand], W=[tops])
            em.op("dve", lambda e, h=h: e.match_replace(out=mr[:], in_to_replace=tops[:, h, 0:8], in_values=cand[:, h, :],
                                                        imm_value=NEG), R=[cand, tops], W=[mr])
            em.op("dve", lambda e, h=h: e.max(out=tops[:, h, 8:16], in_=mr[:]), R=[mr], W=[tops])
            em.op("dve", lambda e, h=h: e.tensor_tensor(
                out=oh, in0=cand[:, h, :].unsqueeze(1).to_broadcast([128, 16, 256]),
                in1=tops[:, h, :].unsqueeze(2).to_broadcast([128, 16, 256]), op=ALU.is_equal), R=[cand, tops], W=[z])
            em.op("dve", lambda e, h=h: e.tensor_tensor(
                out=oh, in0=oh, in1=cidx[:, h, :].unsqueeze(1).to_broadcast([128, 16, 256]), op=ALU.mult),
                R=[z, cidx], W=[z])
            em.op("dve", lambda e, h=h: e.tensor_reduce(out=eidf[:, h * 16:(h + 1) * 16], in_=oh, axis=AX.X, op=ALU.add),
                  R=[z], W=[eidf])
            em.op("dve", lambda e, h=h: e.tensor_scalar(out=small[:, 8 + h:9 + h], in0=tops[:, h, 0:1], scalar1=-1.0,
                                                        scalar2=None, op0=ALU.mult), R=[tops], W=[small])
            em.op("dve", lambda e, h=h: e.memset(zsum[:, h:h + 1], 0.0), W=[zsum])
            em.op("act", lambda e, h=h: e.activation(out=gate[:, h * 16:(h + 1) * 16], in_=tops[:, h, :], func=AF.Exp,
                                                     bias=small[:, 8 + h:9 + h], scale=1.0,
                                                     accum_out=zsum[:, h:h + 1]), R=[tops, small], W=[gate, zsum])
        em.op("dve", lambda e: e.reciprocal(out=zsum[:, 8:16], in_=zsum[:, 0:8]), R=[zsum], W=[zsum])
        em.op("dve", lambda e: e.tensor_tensor(
            out=gate[:].rearrange("p (h k) -> p h k", k=16), in0=gate[:].rearrange("p (h k) -> p h k", k=16),
            in1=zsum[:, 8:16].unsqueeze(2).to_broadcast([128, 8, 16]), op=ALU.mult), R=[gate, zsum], W=[gate])
        em.op("dve", lambda e: e.tensor_scalar(out=eidf[:], in0=eidf[:], scalar1=0.0, scalar2=16383.0,
                                               op0=ALU.max, op1=ALU.min), R=[eidf], W=[eidf])
        if l > 0:
            em.op("dve", lambda e: e.tensor_scalar(out=eidf[:], in0=eidf[:], scalar1=float(l * 16384), scalar2=None,
                                                   op0=ALU.add), R=[eidf], W=[eidf])
        em.op("dve", lambda e: e.tensor_copy(out=eid[:], in_=eidf[:]), R=[eidf], W=[eid])
        em.op("dve", lambda e: e.memset(actv[:], 0.0), W=[actv])
        for j in range(128):
            g = gb[j % 4]
            em.dma("pool", g[:], peer_u, R=[eid], W=[g], fn=lambda e, g=g, j=j: e.indirect_dma_start(
                out=g[:], out_offset=None, in_=peer_u,
                in_offset=bass.IndirectOffsetOnAxis(ap=eid[:, j:j + 1], axis=0)))
            em.op("dve", lambda e, g=g, j=j: e.scalar_tensor_tensor(
                out=g[:], in0=g[:], scalar=1.0, in1=tmA[:], op0=ALU.mult, op1=ALU.mult,
                accum_out=actv[:, j:j + 1]), R=[g, tmA], W=[g, actv])
        em.op("act", lambda e: e.activation(out=ga[:], in_=actv[:], func=AF.Gelu), R=[actv], W=[ga])
        em.op("dve", lambda e: e.tensor_tensor(out=ga[:], in0=ga[:], in1=gate[:], op=ALU.mult), R=[ga, gate], W=[ga])
        for j in range(128):
            g = gb[j % 4]
            em.dma("pool", g[:], peer_v, R=[eid], W=[g], fn=lambda e, g=g, j=j: e.indirect_dma_start(
                out=g[:], out_offset=None, in_=peer_v,
                in_offset=bass.IndirectOffsetOnAxis(ap=eid[:, j:j + 1], axis=0)))
            acc = tmC if j % 2 == 0 else tmB
            if j < 2:
                em.op("dve", lambda e, g=g, j=j, acc=acc: e.tensor_scalar(out=acc[:], in0=g[:], scalar1=ga[:, j:j + 1],
                                                                          scalar2=None, op0=ALU.mult), R=[g, ga], W=[acc])
            else:
                em.op("dve", lambda e, g=g, j=j, acc=acc: e.scalar_tensor_tensor(
                    out=acc[:], in0=g[:], scalar=ga[:, j:j + 1], in1=acc[:], op0=ALU.mult, op1=ALU.add),
                    R=[g, ga, acc], W=[acc])
        em.op("dve", lambda e: e.tensor_tensor(out=tmC[:], in0=tmC[:], in1=tmB[:], op=ALU.add), R=[tmC, tmB], W=[tmC])
        em.op("dve", lambda e: e.tensor_tensor(out=tmC[:], in0=tmC[:], in1=grep[1][:], op=ALU.mult),
              R=[tmC, grep[1]], W=[tmC])
        em.op("dve", lambda e: e.tensor_tensor(out=x[:], in0=x[:], in1=tmC[:], op=ALU.add), R=[x, tmC], W=[x])

    def mix_out(wap):
        def cons_o(p, n0, nb):
            em.op("dve", lambda e: e.tensor_tensor(out=tmC[:, n0:n0 + nb], in0=p[:, 0:nb], in1=grep[0][:, n0:n0 + nb],
                                                   op=ALU.mult), R=[p, grep[0]], W=[tmC])
        linear(wap, D, cons_o)
        em.op("dve", lambda e: e.tensor_tensor(out=x[:], in0=x[:], in1=tmC[:], op=ALU.add), R=[x, tmC], W=[x])

    def gmlp(which, t):
        norm_mod_T(0, which)

        def cons_z(p, n0, nb):
            em.op("act", lambda e: e.activation(out=z[:, n0:n0 + nb], in_=p[:, 0:nb], func=AF.Gelu), R=[p], W=[z])
        linear(a_w_in, 2 * D, cons_z, bias_ap=a_b_in)
        rstd_of(z[:, D:2 * D], z, 1, D)
        em.op("dve", lambda e: e.scalar_tensor_tensor(out=tmA[:], in0=z[:, D:2 * D], scalar=small[:, 1:2], in1=vg_rep[:],
                                                      op0=ALU.mult, op1=ALU.mult), R=[z, small, vg_rep], W=[tmA])
        if t == NA - 1:
            em.dma("sp", gv_o[:, :], tmA[:], R=[tmA], W=[gv_o], ckey="out_gv")
        for g in range(16):
            pw = psW[g // 4]
            em.op("pe", lambda e, g=g, pw=pw: e.matmul(pw[:, (g % 4) * 128:(g % 4 + 1) * 128], lhsT=wsT[:, g, :],
                                                       rhs=tmA[:, g * 128:(g + 1) * 128], start=True, stop=True),
                  R=[wsT, tmA], W=[pw])
        for g in range(16):
            pw = psW[g // 4]
            em.op("dve", lambda e, g=g, pw=pw: e.scalar_tensor_tensor(
                out=tmB[:, g * 128:(g + 1) * 128], in0=pw[:, (g % 4) * 128:(g % 4 + 1) * 128], scalar=bsT[:, g:g + 1],
                in1=z[:, g * 128:(g + 1) * 128], op0=ALU.add, op1=ALU.mult), R=[pw, bsT, z], W=[tmB])
        transpose_to(hT, tmB)
        mix_out(a_w_out)

    def shared_kv(which, t):
        norm_mod_T(2, which)

        def cons_kv(p, n0, nb):
            em.op("act", lambda e: e.copy(out=z[:, n0:n0 + nb], in_=p[:, 0:nb]), R=[p], W=[z])
        linear(kv_w, 2 * D, cons_kv)
        rows = slice(t * 128, (t + 1) * 128)
        em.dma("sp", v_o[rows, :], z[:, D:2 * D], R=[z], W=[v_o], ckey="out_v")
        em.dma("sp", kv_all[rows, D:2 * D], z[:, D:2 * D], R=[z], W=["kv_all"], ckey="kv_all_w")
        head_norm(z[:, 0:D], z, kng, tmA)
        em.dma("sp", k_o[rows, :], tmA[:], R=[tmA], W=[k_o], ckey="out_k")
        em.dma("sp", kv_all[rows, 0:D], tmA[:], R=[tmA], W=["kv_all"], ckey="kv_all_w")

    def sb_block(hg, HG, nq, Kb, Vb, mask, first, last, s, o_fn):
        W = HG * nq
        base = "wsTs" if s == 0 else "vg_rep"
        bt = wsT[:].rearrange("p g t -> p (g t)") if s == 0 else vg_rep[:]
        ez, sp_, lm, a_ = (bt[:, i * 512:i * 512 + W] for i in range(4))
        kez, ksp, klm, ka = (base + "#" + n for n in ("ez", "sp", "lm", "a"))
        tt = att_t[:, s * 512:s * 512 + W]
        ktt = "att_t#%d" % s
        if HG == 4:
            ktk = "cand#kt%d" % s
            ktv = cand[:].rearrange("p h c -> p (h c)")[:, s * 512:(s + 1) * 512].rearrange("p (h k) -> p h k", k=128)
        else:
            ktk = "cand"
            ktv = cand[:].rearrange("p h c -> p (h c)").rearrange("p (h k) -> p h k", k=128)
        transpose_blocks(HG, lambda b: Kb[:, b * 128:(b + 1) * 128], [Kb],
                         lambda g0, n: ktv[:, g0:g0 + n, :], ktk)
        zp = psL[st["pl"]]
        lp = psL[st["pl"] ^ 1] if HG == 16 else psW[1 + s]
        if HG != 16:
            st["pl"] ^= 1
        for i in range(HG):
            em.op("pe", lambda e, i=i: e.matmul(zp[:, i * nq:(i + 1) * nq], lhsT=ktv[:, i, :],
                                                rhs=hT[:, hg * HG + i, 0:nq], start=True, stop=False),
                  R=[ktk, hT], W=[zp])
        brow = biasp[0:1, hg * 512:hg * 512 + W] if HG == 4 else biass[0:1, 0:W]
        em.op("pe", lambda e: e.matmul(zp[:, 0:W], lhsT=ones_row[0:1, :], rhs=brow, start=False, stop=True),
              R=[ones_row, biasp, biass], W=[zp])
        em.op("act", lambda e: e.activation(out=ez, in_=zp[:, 0:W], func=AF.Exp), R=[zp], W=[kez])
        em.op("act", lambda e: e.activation(out=sp_, in_=ez, func=AF.Ln, bias=1.0, scale=1.0), R=[kez], W=[ksp])
        v3 = lambda ap: ap.rearrange("p (h q) -> p h q", q=nq)
        if mask is None:
            em.op("dve", lambda e: e.tensor_scalar(out=lm, in0=sp_, scalar1=-1.0, scalar2=None, op0=ALU.mult),
                  R=[ksp], W=[klm])
        else:
            em.op("dve", lambda e: e.scalar_tensor_tensor(out=v3(lm), in0=v3(sp_), scalar=-1.0, in1=mask,
                                                          op0=ALU.mult, op1=ALU.mult), R=[ksp, amask, smask], W=[klm])
        em.op("pe", lambda e: e.matmul(lp[:, 0:W], lhsT=umat[:], rhs=lm, start=True, stop=first),
              R=[umat, klm], W=[lp])
        if not first:
            em.op("pe", lambda e: e.matmul(lp[:, 0:W], lhsT=ones128[:], rhs=lacc[:, 0:W], start=False, stop=True),
                  R=[ones128, lacc], W=[lp])
        em.op("dve", lambda e: e.tensor_tensor(out=tt, in0=zp[:, 0:W], in1=sp_, op=ALU.subtract), R=[zp, ksp], W=[ktt])
        em.op("dve", lambda e: e.tensor_tensor(out=tt, in0=lp[:, 0:W], in1=tt, op=ALU.add), R=[lp, ktt], W=[ktt])
        em.op("act", lambda e: e.activation(out=a_, in_=tt, func=AF.Exp), R=[ktt], W=[ka])
        if mask is not None:
            em.op("dve", lambda e: e.tensor_tensor(out=v3(a_), in0=v3(a_), in1=mask, op=ALU.mult),
                  R=[ka, amask, smask], W=[ka])
        if not last:
            if first:
                em.op("dve", lambda e: e.tensor_copy(out=lacc[:, 0:W], in_=lm), R=[klm], W=[lacc])
            else:
                em.op("dve", lambda e: e.tensor_tensor(out=lacc[:, 0:W], in0=lacc[:, 0:W], in1=lm, op=ALU.add),
                      R=[lacc, klm], W=[lacc])
        for i in range(HG):
            pt_, oap = o_fn(i)
            em.op("pe", lambda e, i=i, oap=oap: e.matmul(oap, lhsT=a_[:, i * nq:(i + 1) * nq],
                                                         rhs=Vb[:, i * 128:(i + 1) * 128], start=first, stop=last),
                  R=[ka, Vb], W=[pt_])

    def sbp_front(hg, Kb, mask, s):
        W = 512
        base = "wsTs" if s == 0 else "vg_rep"
        bt = wsT[:].rearrange("p g t -> p (g t)") if s == 0 else vg_rep[:]
        ez, sp_, lm, a_ = (bt[:, i * 512:i * 512 + W] for i in range(4))
        kez, ksp, klm, ka = (base + "#" + n for n in ("ez", "sp", "lm", "a"))
        ktk = "cand#kt%d" % s
        ktv = cand[:].rearrange("p h c -> p (h c)")[:, s * 512:(s + 1) * 512].rearrange("p (h k) -> p h k", k=128)
        transpose_blocks(4, lambda b: Kb[:, b * 128:(b + 1) * 128], [Kb], lambda g0, n: ktv[:, g0:g0 + n, :], ktk)
        zp = psL[s]
        for i in range(4):
            em.op("pe", lambda e, i=i: e.matmul(zp[:, i * 128:(i + 1) * 128], lhsT=ktv[:, i, :],
                                                rhs=hT[:, hg * 4 + i, :], start=True, stop=False), R=[ktk, hT], W=[zp])
        em.op("pe", lambda e: e.matmul(zp[:, 0:W], lhsT=ones_row[0:1, :], rhs=biasp[0:1, hg * 512:(hg + 1) * 512],
                                       start=False, stop=True), R=[ones_row, biasp], W=[zp])
        em.op("act", lambda e: e.activation(out=ez, in_=zp[:, 0:W], func=AF.Exp), R=[zp], W=[kez])
        em.op("act", lambda e: e.activation(out=sp_, in_=ez, func=AF.Ln, bias=1.0, scale=1.0), R=[kez], W=[ksp])
        v3 = lambda ap: ap.rearrange("p (h q) -> p h q", q=128)
        if mask is None:
            em.op("dve", lambda e: e.tensor_scalar(out=lm, in0=sp_, scalar1=-1.0, scalar2=None, op0=ALU.mult),
                  R=[ksp], W=[klm])
        else:
            em.op("dve", lambda e: e.scalar_tensor_tensor(out=v3(lm), in0=v3(sp_), scalar=-1.0, in1=mask,
                                                          op0=ALU.mult, op1=ALU.mult), R=[ksp, amask], W=[klm])
        return dict(s=s, zp=zp, sp_=sp_, lm=lm, a_=a_, ksp=ksp, klm=klm, ka=ka, mask=mask, v3=v3)

    def sbp_back(c, Vb, first, last):
        W = 512
        s, zp, sp_, lm, a_, ksp, klm, ka, mask, v3 = (c[k] for k in ("s", "zp", "sp_", "lm", "a_", "ksp", "klm", "ka", "mask", "v3"))
        lp = psW[1 + s]
        tt = att_t[:, s * 512:s * 512 + W]
        ktt = "att_t#%d" % s
        em.op("pe", lambda e: e.matmul(lp[:, 0:W], lhsT=umat[:], rhs=lm, start=True, stop=first), R=[umat, klm], W=[lp])
        if not first:
            em.op("pe", lambda e: e.matmul(lp[:, 0:W], lhsT=ones128[:], rhs=lacc[:, 0:W], start=False, stop=True),
                  R=[ones128, lacc], W=[lp])
        em.op("dve", lambda e: e.tensor_tensor(out=tt, in0=zp[:, 0:W], in1=sp_, op=ALU.subtract), R=[zp, ksp], W=[ktt])
        em.op("dve", lambda e: e.tensor_tensor(out=tt, in0=lp[:, 0:W], in1=tt, op=ALU.add), R=[lp, ktt], W=[ktt])
        em.op("act", lambda e: e.activation(out=a_, in_=tt, func=AF.Exp), R=[ktt], W=[ka])
        if mask is not None:
            em.op("dve", lambda e: e.tensor_tensor(out=v3(a_), in0=v3(a_), in1=mask, op=ALU.mult), R=[ka, amask], W=[ka])
        if not last:
            if first:
                em.op("dve", lambda e: e.tensor_copy(out=lacc[:, 0:W], in_=lm), R=[klm], W=[lacc])
            else:
                em.op("dve", lambda e: e.tensor_tensor(out=lacc[:, 0:W], in0=lacc[:, 0:W], in1=lm, op=ALU.add),
                      R=[lacc, klm], W=[lacc])
        for i in range(4):
            em.op("pe", lambda e, i=i: e.matmul(psW[0][:, i * 128:(i + 1) * 128], lhsT=a_[:, i * 128:(i + 1) * 128],
                                                rhs=Vb[:, i * 128:(i + 1) * 128], start=first, stop=last),
                  R=[ka, Vb], W=[psW[0]])

    def attn_prompt(j):
        nk = 4 * j + 4
        steps = [(hg, n, kc) for hg in range(4) for n, kc in enumerate(range(nk - 1, -1, -1))]

        def front(idx):
            hg, n, kc = steps[idx]
            s = idx % 2
            Kb, Vb = gb[s], gb[2 + s]
            rows = slice(kc * 128, (kc + 1) * 128)
            em.dma("sp", Kb[:, 0:512], kv_all[rows, hg * 512:(hg + 1) * 512], R=["kv_all"], W=[Kb])
            em.dma("sp", Vb[:, 0:512], kv_all[rows, D + hg * 512:D + (hg + 1) * 512], R=["kv_all"], W=[Vb])
            d = kc - 4 * j
            mask = None if d < 0 else amask[:, d * 128:(d + 1) * 128].unsqueeze(1).to_broadcast([128, 4, 128])
            return sbp_front(hg, Kb, mask, s), Vb

        nxt = front(0)
        for idx, (hg, n, kc) in enumerate(steps):
            cur = nxt
            if idx + 1 < len(steps):
                nxt = front(idx + 1)
            sbp_back(cur[0], cur[1], n == 0, n == nk - 1)
            if n == nk - 1:
                em.op("act", lambda e, hg=hg: e.copy(out=tmB[:, hg * 512:(hg + 1) * 512], in_=psW[0][:]),
                      R=[psW[0]], W=[tmB])

    def attn_sample():
        W = 64
        blocks = [None] + list(range(127, -1, -1))
        ktv = cand[:].rearrange("p h c -> p (h c)").rearrange("p (h k) -> p h k", k=128)
        v3 = lambda ap: ap.rearrange("p (h q) -> p h q", q=4)

        def front(n):
            pg = blocks[n]
            s = n % 2
            Kb, Vb = gb[s], gb[2 + s]
            if pg is None:
                rows = slice((NA - 1) * 128, NA * 128)
                em.dma("sp", Kb[:], kv_all[rows, 0:D], R=["kv_all"], W=[Kb])
                em.dma("sp", Vb[:], kv_all[rows, D:2 * D], R=["kv_all"], W=[Vb])
                mask = smask[:].rearrange("p (h q) -> p h q", q=4)
            else:
                for buf, src in ((Kb, cache_k), (Vb, cache_v)):
                    em.dma("pool", buf[:], src, R=[pidx], W=[buf], fn=lambda e, buf=buf, src=src, pg=pg:
                           e.indirect_dma_start(out=buf[:], out_offset=None, in_=src,
                                                in_offset=bass.IndirectOffsetOnAxis(ap=pidx[:, pg:pg + 1], axis=0)))
                mask = None
            base = "wsTs" if s == 0 else "vg_rep"
            bt = wsT[:].rearrange("p g t -> p (g t)") if s == 0 else vg_rep[:]
            ez, sp_, lm, a_ = (bt[:, i * 512:i * 512 + W] for i in range(4))
            kez, ksp, klm, ka = (base + "#" + nm for nm in ("ez", "sp", "lm", "a"))
            transpose_blocks(16, lambda b: Kb[:, b * 128:(b + 1) * 128], [Kb], lambda g0, n_: ktv[:, g0:g0 + n_, :], "cand")
            zp = psL[0][:, s * 128:s * 128 + W]
            kzp = "psL0#z%d" % s
            for i in range(16):
                em.op("pe", lambda e, i=i: e.matmul(zp[:, i * 4:(i + 1) * 4], lhsT=ktv[:, i, :], rhs=hT[:, i, 0:4],
                                                    start=True, stop=False), R=["cand", hT], W=[kzp])
            em.op("pe", lambda e: e.matmul(zp, lhsT=ones_row[0:1, :], rhs=biass[0:1, 0:W], start=False, stop=True),
                  R=[ones_row, biass], W=[kzp])
            em.op("act", lambda e: e.activation(out=ez, in_=zp, func=AF.Exp), R=[kzp], W=[kez])
            em.op("act", lambda e: e.activation(out=sp_, in_=ez, func=AF.Ln, bias=1.0, scale=1.0), R=[kez], W=[ksp])
            if mask is None:
                em.op("dve", lambda e: e.tensor_scalar(out=lm, in0=sp_, scalar1=-1.0, scalar2=None, op0=ALU.mult),
                      R=[ksp], W=[klm])
            else:
                em.op("dve", lambda e: e.scalar_tensor_tensor(out=v3(lm), in0=v3(sp_), scalar=-1.0, in1=mask,
                                                              op0=ALU.mult, op1=ALU.mult), R=[ksp, smask], W=[klm])
            return dict(s=s, zp=zp, kzp=kzp, sp_=sp_, lm=lm, a_=a_, ksp=ksp, klm=klm, ka=ka, mask=mask, Vb=Vb)

        def back(c, first, last):
            s, zp, kzp, sp_, lm, a_, ksp, klm, ka, mask, Vb = (c[k] for k in
                                                               ("s", "zp", "kzp", "sp_", "lm", "a_", "ksp", "klm", "ka", "mask", "Vb"))
            lp = psL[1][:, s * 128:s * 128 + W]
            klp = "psL1#l%d" % s
            tt = att_t[:, s * 512:s * 512 + W]
            ktt = "att_t#%d" % s
            em.op("pe", lambda e: e.matmul(lp, lhsT=umat[:], rhs=lm, start=True, stop=first), R=[umat, klm], W=[klp])
            if not first:
                em.op("pe", lambda e: e.matmul(lp, lhsT=ones128[:], rhs=lacc[:, 0:W], start=False, stop=True),
                      R=[ones128, lacc], W=[klp])
            em.op("dve", lambda e: e.tensor_tensor(out=tt, in0=zp, in1=sp_, op=ALU.subtract), R=[kzp, ksp], W=[ktt])
            em.op("dve", lambda e: e.tensor_tensor(out=tt, in0=lp, in1=tt, op=ALU.add), R=[klp, ktt], W=[ktt])
            em.op("act", lambda e: e.activation(out=a_, in_=tt, func=AF.Exp), R=[ktt], W=[ka])
            if mask is not None:
                em.op("dve", lambda e: e.tensor_tensor(out=v3(a_), in0=v3(a_), in1=mask, op=ALU.mult), R=[ka, smask], W=[ka])
            if not last:
                if first:
                    em.op("dve", lambda e: e.tensor_copy(out=lacc[:, 0:W], in_=lm), R=[klm], W=[lacc])
                else:
                    em.op("dve", lambda e: e.tensor_tensor(out=lacc[:, 0:W], in0=lacc[:, 0:W], in1=lm, op=ALU.add),
                          R=[lacc, klm], W=[lacc])
            for i in range(16):
                pw = psW[i // 4]
                em.op("pe", lambda e, i=i, pw=pw: e.matmul(pw[0:4, (i % 4) * 128:(i % 4 + 1) * 128],
                                                           lhsT=a_[:, i * 4:(i + 1) * 4], rhs=Vb[:, i * 128:(i + 1) * 128],
                                                           start=first, stop=last), R=[ka, Vb], W=[pw])

        nxt = front(0)
        for n in range(len(blocks)):
            cur = nxt
            if n + 1 < len(blocks):
                nxt = front(n + 1)
            back(cur, n == 0, n == len(blocks) - 1)
        em.op("dve", lambda e: e.memset(tmB[:], 0.0), W=[tmB])
        for q4 in range(4):
            em.op("act", lambda e, q4=q4: e.copy(out=tmB[0:4, q4 * 512:(q4 + 1) * 512], in_=psW[q4][0:4, :]),
                  R=[psW[q4]], W=[tmB])

    def layer1(which, attn):
        norm_mod_T(3, which)

        def cons_q(p, n0, nb):
            em.op("act", lambda e: e.copy(out=z[:, n0:n0 + nb], in_=p[:, 0:nb]), R=[p], W=[z])
        linear(b_w_q, D, cons_q)
        head_norm(z[:, 0:D], z, qng, tmA, extra=128.0 ** -0.5)
        transpose_to(hT, tmA)
        attn()
        transpose_to(hT, tmB)
        mix_out(b_w_o)
        peer(1, which, 4)

    for which, tiles in ((0, range(0, NP)), (1, range(NP, NA))):
        gate_rep(0, which, 2, grep[0])
        gate_rep(0, which, 5, grep[1])
        for t in tiles:
            src = xall[t * 128:(t + 1) * 128, :] if t < NP else xsmp
            em.dma("sp", x[:], src, R=[], W=[x])
            gmlp(which, t)
            peer(0, which, 1)
            shared_kv(which, t)
            em.dma("sp", xs_all[t * 128:(t + 1) * 128, :], x[:], R=[x], W=["xs_all"], ckey="xs_all_w")
    for which, tiles in ((0, range(0, 8)), (1, range(8, 9))):
        gate_rep(1, which, 2, grep[0])
        gate_rep(1, which, 5, grep[1])
        for t in tiles:
            if t < 8:
                em.dma("pool", x[:], xs_all, R=["xs_all", myrows], W=[x], fn=lambda e, t=t: e.indirect_dma_start(
                    out=x[:], out_offset=None, in_=xs_all,
                    in_offset=bass.IndirectOffsetOnAxis(ap=myrows[:, t:t + 1], axis=0)))
                layer1(which, lambda t=t: attn_prompt(t))
            else:
                em.dma("sp", x[:], xs_all[(NA - 1) * 128:NA * 128, :], R=["xs_all"], W=[x])
                layer1(which, attn_sample)
            em.dma("sp", y_o[t * 128:(t + 1) * 128, :], x[:], R=[x], W=[y_o], ckey="out_y")

    em.wait_all_dma("sp")
    em.replay()
    return nc, es


def _fm(vec):
    return np.ascontiguousarray(vec.reshape(-1, 128).T)


def kernel(x_prompt, x_sample, cache_k, cache_v, page_table, c_prompt, c_sample,
           mod_w, mod_b, norm_mix_g, norm_ffn_g,
           a_w_in, a_b_in, a_v_norm_g, a_w_s, a_b_s, a_w_out,
           kv_mod_w, kv_mod_b, kv_norm_g, kv_w, k_norm_g,
           b_w_q, b_q_norm_g, b_logit_bias, b_w_o,
           peer_w_q, peer_subkeys, peer_u, peer_v):
    f = lambda a: np.ascontiguousarray(np.asarray(a, dtype=np.float32))
    x_prompt, x_sample, c_prompt, c_sample = f(x_prompt), f(x_sample), f(c_prompt), f(c_sample)
    mod_w, mod_b = f(mod_w), f(mod_b)
    lb = f(b_logit_bias)[0]
    kq = np.arange(128)
    shared = {
        "mod_w": mod_w, "mod_b": mod_b,
        "mod_bT": np.stack([_fm(mod_b[l]) for l in range(2)]),
        "normT": np.concatenate([_fm(f(norm_mix_g)[0]), _fm(f(norm_ffn_g)[0]), _fm(f(kv_norm_g)),
                                 _fm(f(norm_mix_g)[1]), _fm(f(norm_ffn_g)[1])], axis=1),
        "a_w_in": f(a_w_in)[0], "a_b_in": f(a_b_in)[0:1], "a_vg": f(a_v_norm_g)[0:1],
        "wsT": np.ascontiguousarray(np.transpose(f(a_w_s)[0], (2, 0, 1)).reshape(128, 16 * 128)),
        "bsT": np.ascontiguousarray(f(a_b_s)[0].T),
        "a_w_out": f(a_w_out)[0],
        "kv_mod_w": f(kv_mod_w), "kv_mod_bT": _fm(f(kv_mod_b)), "kv_w": f(kv_w),
        "k_norm_g": f(k_norm_g).reshape(1, 128), "q_norm_g": f(b_q_norm_g)[0].reshape(1, 128),
        "b_w_q": f(b_w_q)[0], "b_w_o": f(b_w_o)[0],
        "peer_w_q": f(peer_w_q),
        "skT": np.ascontiguousarray(np.transpose(f(peer_subkeys), (0, 3, 1, 2)).reshape(2, 128, 256)),
        "peer_u": f(peer_u).reshape(2 * 16384, D), "peer_v": f(peer_v).reshape(2 * 16384, D),
        "cache_k": f(cache_k).reshape(1280 * 128, D), "cache_v": f(cache_v).reshape(1280 * 128, D),
        "ident": np.eye(128, dtype=np.float32),
        "tril": np.triu(np.ones((128, 128), np.float32)),
        "umat": np.tril(np.ones((128, 128), np.float32), -1),
        "ones128": np.ones((128, 128), np.float32),
        "smask": np.ascontiguousarray(np.broadcast_to((kq[:, None] < np.arange(4)[None, :]).astype(np.float32)[:, None, :],
                                                      (128, 16, 4)).reshape(128, 64)),
        "biasrow_p": np.ascontiguousarray(np.repeat(lb, 128).reshape(1, D)),
        "biasrow_s": np.ascontiguousarray(np.repeat(lb, 4).reshape(1, 64)),
        "iota_p": np.arange(128, dtype=np.float32).reshape(128, 1),
    }
    strict = (kq[:, None] < kq[None, :]).astype(np.float32)
    in_maps = []
    for c in range(8):
        b, r = c // 4, c % 4
        xs = np.zeros((128, D), np.float32)
        xs[0:4] = x_sample[c]
        am = np.zeros((128, 4, 128), np.float32)
        for d in range(4):
            am[:, d, :] = 1.0 if d < r else (strict if d == r else 0.0)
        m = dict(shared)
        m["xall"] = x_prompt[b]
        m["xsmp"] = xs
        m["c2T"] = np.ascontiguousarray(np.stack([_fm(c_prompt[b]), _fm(c_sample[c])], axis=2).reshape(128, KT * 2))
        m["pt"] = np.ascontiguousarray(np.asarray(page_table, dtype=np.int32)[c].reshape(1, 128))
        m["myrows"] = np.ascontiguousarray(((4 * np.arange(8)[None, :] + r) * 128 + kq[:, None]).astype(np.int32))
        m["amask"] = np.ascontiguousarray(am.reshape(128, 512))
        in_maps.append(m)

    nc, es = build_program()
    with es:
        res = run_bass_kernel_spmd(nc, in_maps, core_ids=list(range(8)))
    R = res.results

    y_prompt = np.zeros((2, 4096, D), np.float32)
    k_prompt = np.zeros((2, 4096, 16, 128), np.float32)
    v_prompt = np.zeros((2, 4096, 16, 128), np.float32)
    y_sample = np.zeros((8, 4, D), np.float32)
    k_sample = np.zeros((8, 4, 16, 128), np.float32)
    v_sample = np.zeros((8, 4, 16, 128), np.float32)
    gmlp_v = np.zeros((1, 8, 4, D), np.float32)
    for c in range(8):
        b, r = c // 4, c % 4
        y, ko, vo, gv = (np.asarray(R[c][k]) for k in ("y", "ko", "vo", "gv"))
        for j in range(8):
            ch = 4 * j + r
            y_prompt[b, ch * 128:(ch + 1) * 128] = y[j * 128:(j + 1) * 128]
            k_prompt[b, ch * 128:(ch + 1) * 128] = ko[ch * 128:(ch + 1) * 128].reshape(128, 16, 128)
            v_prompt[b, ch * 128:(ch + 1) * 128] = vo[ch * 128:(ch + 1) * 128].reshape(128, 16, 128)
        y_sample[c] = y[1024:1028]
        k_sample[c] = ko[NP * 128:NP * 128 + 4].reshape(4, 16, 128)
        v_sample[c] = vo[NP * 128:NP * 128 + 4].reshape(4, 16, 128)
        gmlp_v[0, c] = gv[0:4]
    return (y_prompt, y_sample, k_prompt, v_prompt, k_sample, v_sample, gmlp_v)
```
